# Optimizing a Trainium2 kernel written in Bass

```python
import math, functools
import jax, jax.numpy as jnp
from jax import lax
import numpy as np


D_MODEL = 1024
BATCH = 16
SEQ = 256
DEPTH = 2
DEC_BATCH = 8
DEC_SEQ = 2048
PAST_LEN = 256

GRID_W = 64
H_A = 4
DH_A = 64
H_B = 8
KV_B = 2
G_B = H_B // KV_B
DH_B = 64
WINDOW = 128
WIN_BLOCK = 128
C_CONV = 512
CONV_K = 31
BRANCH_W = 512
D_FF = 2816
Q_BLOCK = 128
ROPE_BASE = 10000.0
NORM_EPS = 1e-6
SUBLN_EPS = 1e-5
COL_SIZES = (H_A * 2 * DH_A, H_A * 2 * DH_A, H_A * 2 * DH_A,
             H_B * DH_B, KV_B * DH_B, KV_B * DH_B,
             2 * C_CONV, D_MODEL, D_MODEL, D_MODEL)
D_IN = sum(COL_SIZES)
SCALE_A = DH_A ** -0.5
SCALE_B = DH_B ** -0.5

kernel_name = 'hybrid_diffusion_prefix_step'


def rmsnorm(x, g, eps=NORM_EPS):
    xf = x.astype(jnp.float32)
    y = xf * lax.rsqrt(jnp.mean(xf * xf, axis=-1, keepdims=True) + eps)
    return (y * g.astype(jnp.float32)).astype(x.dtype)


def layernorm(x, g, b, eps=NORM_EPS):
    xf = x.astype(jnp.float32)
    mu = jnp.mean(xf, axis=-1, keepdims=True)
    var = jnp.mean(jnp.square(xf - mu), axis=-1, keepdims=True)
    y = (xf - mu) * lax.rsqrt(var + eps) * g.astype(jnp.float32) + b.astype(jnp.float32)
    return y.astype(x.dtype)


def modulated_rmsnorm(x, g, shift, scale):
    return rmsnorm(x, g) * (1 + scale) + shift


def swiglu(x, w_in, w_out):
    a, b = jnp.split(x @ w_in, 2, axis=-1)
    return (jax.nn.silu(a) * b) @ w_out


def split_columns(p):
    idx = np.cumsum(COL_SIZES)[:-1].tolist()
    return jnp.split(p, idx, axis=-1)


def axial_rope_table(n_tokens, dim, dtype):
    n_rows = n_tokens // GRID_W
    rows, cols = jnp.meshgrid(jnp.arange(n_rows, dtype=jnp.float32),
                              jnp.arange(GRID_W, dtype=jnp.float32), indexing='ij')
    inv = ROPE_BASE ** (-jnp.arange(0, dim // 2, 2, dtype=jnp.float32) / (dim // 2))
    ang = jnp.stack([rows.reshape(-1, 1) * inv, cols.reshape(-1, 1) * inv], axis=1)
    ang = jnp.broadcast_to(ang[:, :, None, :], (n_tokens, 2, 2, dim // 4)).reshape(n_tokens, dim)
    return jnp.cos(ang).astype(dtype), jnp.sin(ang).astype(dtype)


def apply_rope(x, cos, sin):
    dim = x.shape[-1]
    x4 = x.reshape(x.shape[:-1] + (2, 2, dim // 4))
    rot = jnp.stack([-x4[..., 1, :], x4[..., 0, :]], axis=-2).reshape(x.shape)
    return x * cos[:, None, :] + rot * sin[:, None, :]


def map_query_blocks(fn, q):
    B, T = q.shape[:2]
    nb = T // Q_BLOCK
    qb = jnp.moveaxis(q.reshape((B, nb, Q_BLOCK) + q.shape[2:]), 1, 0)
    out = lax.map(fn, qb)
    return jnp.moveaxis(out, 0, 1).reshape((B, T) + out.shape[3:])


def diff_attention(q, k, v, lam_full, lam_init, subln_g):
    B, T = q.shape[:2]

    def blk(qn):
        s = jnp.einsum('bqhcd,bshcd->bhcqs', qn, k).astype(jnp.float32) * SCALE_A
        p = jax.nn.softmax(s, axis=-1)
        a = (p[:, :, 0] - lam_full * p[:, :, 1]).astype(v.dtype)
        return jnp.einsum('bhqs,bshe->bqhe', a, v)

    o = map_query_blocks(blk, q)
    o = rmsnorm(o, subln_g, SUBLN_EPS) * (1.0 - lam_init)
    return o.reshape(B, T, H_A * 2 * DH_A)


def sink_attention_dense(q, k, v, sink):
    B, L = q.shape[:2]
    sink_col = sink.astype(jnp.float32).reshape(1, KV_B, G_B, 1, 1)

    def blk(qn):
        s = jnp.einsum('bqkgd,bskd->bkgqs', qn, k).astype(jnp.float32) * SCALE_B
        s = jnp.concatenate([s, jnp.broadcast_to(sink_col, s.shape[:-1] + (1,))], axis=-1)
        p = jax.nn.softmax(s, axis=-1)[..., :-1].astype(v.dtype)
        return jnp.einsum('bkgqs,bskd->bqkgd', p, v)

    o = map_query_blocks(blk, q)
    return o.reshape(B, L, H_B * DH_B)


def window_sink_attention(q, k, v, ck, cv, sink):
    B, T = q.shape[:2]
    nb = T // WIN_BLOCK
    pad = [(0, 0), (WIN_BLOCK, WIN_BLOCK), (0, 0), (0, 0)]
    kp = jnp.pad(k, pad)
    vp = jnp.pad(v, pad)
    qb = jnp.moveaxis(q.reshape(B, nb, WIN_BLOCK, KV_B, G_B, DH_B), 1, 0)
    sink_col = jnp.broadcast_to(sink.astype(jnp.float32).reshape(1, KV_B, G_B, 1, 1),
                                (B, KV_B, G_B, WIN_BLOCK, 1))
    offs_q = jnp.arange(WIN_BLOCK)
    offs_k = jnp.arange(3 * WIN_BLOCK) - WIN_BLOCK
    n_loc = 3 * WIN_BLOCK
    n_ctx = ck.shape[1]

    def blk(args):
        qn, n = args
        start = n * WIN_BLOCK
        kn = lax.dynamic_slice_in_dim(kp, start, n_loc, axis=1)
        vn = lax.dynamic_slice_in_dim(vp, start, n_loc, axis=1)
        qpos = start + offs_q
        kpos = start + offs_k
        valid = ((jnp.abs(kpos[None, :] - qpos[:, None]) <= WINDOW)
                 & (kpos >= 0)[None, :] & (kpos < T)[None, :])
        s_loc = jnp.einsum('bqkgd,bskd->bkgqs', qn, kn).astype(jnp.float32) * SCALE_B
        s_loc = jnp.where(valid, s_loc, -jnp.inf)
        s_ctx = jnp.einsum('bqkgd,bskd->bkgqs', qn, ck).astype(jnp.float32) * SCALE_B
        p = jax.nn.softmax(jnp.concatenate([s_loc, s_ctx, sink_col], axis=-1), axis=-1).astype(v.dtype)
        return (jnp.einsum('bkgqs,bskd->bqkgd', p[..., :n_loc], vn)
                + jnp.einsum('bkgqs,bskd->bqkgd', p[..., n_loc:n_loc + n_ctx], cv))

    o = lax.map(blk, (qb, jnp.arange(nb)))
    return jnp.moveaxis(o, 0, 1).reshape(B, T, H_B * DH_B)


def conv_module(u, w_dw, b_dw, g, b):
    a, gate = jnp.split(u, 2, axis=-1)
    u = a * jax.nn.sigmoid(gate)
    pad = CONV_K // 2
    y = lax.conv_general_dilated(u, w_dw.astype(u.dtype)[:, None, :], window_strides=(1,),
                                 padding=[(pad, pad)], dimension_numbers=('NWC', 'WIO', 'NWC'),
                                 feature_group_count=C_CONV) + b_dw
    return jax.nn.silu(layernorm(y, g, b))


def mix_context(parts, lam_full, lam_init, subln_g, sink, conv_params):
    qa, ka, va, qb, kb, vb, cu = parts
    B, L = qa.shape[:2]
    qa = qa.reshape(B, L, H_A, 2, DH_A)
    ka = ka.reshape(B, L, H_A, 2, DH_A)
    va = va.reshape(B, L, H_A, 2 * DH_A)
    kb = kb.reshape(B, L, KV_B, DH_B)
    vb = vb.reshape(B, L, KV_B, DH_B)
    ya = diff_attention(qa, ka, va, lam_full, lam_init, subln_g)
    yb = sink_attention_dense(qb.reshape(B, L, KV_B, G_B, DH_B), kb, vb, sink)
    yc = conv_module(cu, *conv_params)
    return ya, yb, yc, (ka, va, kb, vb)


def mix_latent(parts, ctx_dk, ctx_dv, ctx_wk, ctx_wv, cos, sin, lam_full, lam_init, subln_g, sink,
               conv_params):
    qa, ka, va, qb, kb, vb, cu = parts
    B, T = qa.shape[:2]
    qa = apply_rope(qa.reshape(B, T, 2 * H_A, DH_A), cos, sin).reshape(B, T, H_A, 2, DH_A)
    ka = apply_rope(ka.reshape(B, T, 2 * H_A, DH_A), cos, sin).reshape(B, T, H_A, 2, DH_A)
    va = va.reshape(B, T, H_A, 2 * DH_A)
    ya = diff_attention(qa, jnp.concatenate([ctx_dk, ka], axis=1),
                        jnp.concatenate([ctx_dv, va], axis=1), lam_full, lam_init, subln_g)
    qb = apply_rope(qb.reshape(B, T, H_B, DH_B), cos, sin).reshape(B, T, KV_B, G_B, DH_B)
    kb = apply_rope(kb.reshape(B, T, KV_B, DH_B), cos, sin)
    vb = vb.reshape(B, T, KV_B, DH_B)
    yb = window_sink_attention(qb, kb, vb, ctx_wk, ctx_wv, sink)
    yc = conv_module(cu, *conv_params)
    return ya, yb, yc, None


def run_layer(x, cond, w_mod, b_mod, norm_g, ffn_w_in, ffn_w_out, w_in, w_branch, w_out, mix):
    mod = cond @ w_mod + b_mod
    sh1, sc1, g1, sh2, sc2, g2, sh3, sc3, g3 = [m[:, None, :] for m in jnp.split(mod, 9, axis=-1)]
    h = x + 0.5 * g1 * swiglu(modulated_rmsnorm(x, norm_g[0], sh1, sc1), ffn_w_in[0], ffn_w_out[0])
    parts = split_columns(modulated_rmsnorm(h, norm_g[1], sh2, sc2) @ w_in)
    ya, yb, yc, ctx_tensors = mix(parts[:7])
    ga, gb, gc = parts[7:]
    merged = (jax.nn.sigmoid(ga) * (ya @ w_branch[0]) + jax.nn.sigmoid(gb) * (yb @ w_branch[1])
              + jax.nn.sigmoid(gc) * (yc @ w_branch[2]))
    h = h + g2 * (merged @ w_out)
    h = h + 0.5 * g3 * swiglu(modulated_rmsnorm(h, norm_g[2], sh3, sc3), ffn_w_in[1], ffn_w_out[1])
    return h, ctx_tensors


def setup_inputs(seed: int = 0) -> dict:
    key = jax.random.key(seed)
    ks = list(jax.random.split(key, 32))

    def nrm(i, shape, s):
        return jax.random.normal(ks[i], shape, jnp.float32) * s

    D = D_MODEL
    return {
        'x_prompt': nrm(0, (BATCH, SEQ, D), 1.0),
        'x_sample': nrm(1, (DEC_BATCH, DEC_SEQ, D), 1.0),
        'cache_diff_k': nrm(2, (DEC_BATCH, DEPTH, PAST_LEN, H_A, 2, DH_A), 1.0),
        'cache_diff_v': nrm(3, (DEC_BATCH, DEPTH, PAST_LEN, H_A, 2 * DH_A), 1.0),
        'cache_win_k': nrm(4, (DEC_BATCH, DEPTH, PAST_LEN, KV_B, DH_B), 1.0),
        'cache_win_v': nrm(5, (DEC_BATCH, DEPTH, PAST_LEN, KV_B, DH_B), 1.0),
        'c': nrm(6, (DEC_BATCH, D), 1.0),
        'c_ctx': nrm(7, (D,), 1.0),
        'w_mod': nrm(8, (DEPTH, D, 9 * D), 0.5 * D ** -0.5),
        'b_mod': nrm(9, (DEPTH, 9 * D), 0.01),
        'norm_g': 1.0 + nrm(10, (DEPTH, 3, D), 0.02),
        'ffn_w_in': nrm(11, (DEPTH, 2, D, 2 * D_FF), D ** -0.5),
        'ffn_w_out': nrm(12, (DEPTH, 2, D_FF, D), D_FF ** -0.5),
        'w_in': nrm(13, (DEPTH, D, D_IN), D ** -0.5),
        'diff_lambda': nrm(14, (DEPTH, 4, DH_A), 0.1),
        'diff_subln_g': 1.0 + nrm(15, (DEPTH, 2 * DH_A), 0.02),
        'win_sink': nrm(16, (DEPTH, H_B), 0.5),
        'conv_dw_w': nrm(17, (DEPTH, CONV_K, C_CONV), CONV_K ** -0.5),
        'conv_dw_b': nrm(18, (DEPTH, C_CONV), 0.01),
        'conv_norm_g': 1.0 + nrm(19, (DEPTH, C_CONV), 0.02),
        'conv_norm_b': nrm(20, (DEPTH, C_CONV), 0.01),
        'w_branch': nrm(21, (DEPTH, 3, BRANCH_W, D), BRANCH_W ** -0.5),
        'w_out': nrm(22, (DEPTH, D, D), D ** -0.5),
        'final_norm_g': 1.0 + nrm(23, (D,), 0.02),
    }


def reference(x_prompt, x_sample, cache_diff_k, cache_diff_v, cache_win_k, cache_win_v, c, c_ctx,
              w_mod, b_mod, norm_g, ffn_w_in, ffn_w_out, w_in, diff_lambda, diff_subln_g, win_sink,
              conv_dw_w, conv_dw_b, conv_norm_g, conv_norm_b, w_branch, w_out, final_norm_g):
    T = x_sample.shape[1]
    cos, sin = axial_rope_table(T, DH_A, x_sample.dtype)
    cond_ctx = jax.nn.silu(c_ctx)[None, :]
    cond_lat = jax.nn.silu(c)
    xp, xs = x_prompt, x_sample
    dks, dvs, wks, wvs = [], [], [], []
    for l in range(DEPTH):
        lam = diff_lambda[l].astype(jnp.float32)
        lam_init = 0.8 - 0.6 * math.exp(-0.3 * l)
        lam_full = jnp.exp(jnp.sum(lam[0] * lam[1])) - jnp.exp(jnp.sum(lam[2] * lam[3])) + lam_init
        conv_p = (conv_dw_w[l], conv_dw_b[l], conv_norm_g[l], conv_norm_b[l])
        shared = (w_mod[l], b_mod[l], norm_g[l], ffn_w_in[l], ffn_w_out[l], w_in[l], w_branch[l], w_out[l])
        ctx_mix = functools.partial(mix_context, lam_full=lam_full, lam_init=lam_init,
                                    subln_g=diff_subln_g[l], sink=win_sink[l], conv_params=conv_p)
        xp, (dk, dv, wk, wv) = run_layer(xp, cond_ctx, *shared, ctx_mix)
        dks.append(dk)
        dvs.append(dv)
        wks.append(wk)
        wvs.append(wv)
        lat_mix = functools.partial(mix_latent, ctx_dk=cache_diff_k[:, l], ctx_dv=cache_diff_v[:, l],
                                    ctx_wk=cache_win_k[:, l], ctx_wv=cache_win_v[:, l], cos=cos, sin=sin,
                                    lam_full=lam_full, lam_init=lam_init, subln_g=diff_subln_g[l],
                                    sink=win_sink[l], conv_params=conv_p)
        xs, _ = run_layer(xs, cond_lat, *shared, lat_mix)
    y_prompt = rmsnorm(xp, final_norm_g)
    y_sample = rmsnorm(xs, final_norm_g)
    new_diff_k = jnp.stack(dks, axis=1)
    new_diff_v = jnp.stack(dvs, axis=1)
    new_win_k = jnp.stack(wks, axis=1)
    new_win_v = jnp.stack(wvs, axis=1)
    return (y_prompt, y_sample, new_diff_k, new_diff_v, new_win_k, new_win_v)
```

```python
import math
from contextlib import ExitStack

import numpy as np
import concourse.bass as bass
import concourse.mybir as mybir
from concourse.bass_utils import run_bass_kernel_spmd

F32 = mybir.dt.float32
BF16 = mybir.dt.bfloat16
AF = mybir.ActivationFunctionType
ALU = mybir.AluOpType

D = 1024
KC = 8
NT = 5
TS = 2048
TOK = 2560
DFF = 2816
NG = 11
DIN = 6400
EPS = 1e-6
SUBEPS = 1e-5
SCALE = 0.125
NCORES = 8
MAX_PHASES = 1000
HALO = 15
CONVK = 31

C_QA, C_KA, C_VA, C_QB, C_KB, C_VB, C_CU, C_G = 0, 512, 1024, 1536, 2048, 2176, 2304, 3328

OFF_X = 0
OFF_N = 81920
OFF_C = 122880
OFF_RING = 130048
OFF_L = OFF_RING + 20480
ARENA_BYTES = 212480
LOCAL_BYTES = ARENA_BYTES - OFF_L


class Res:
    __slots__ = ("name", "w", "r", "excl")

    def __init__(self, name="", excl=False):
        self.name = name
        self.w = None
        self.r = {}
        self.excl = excl


class DSem:
    def __init__(self, sem):
        self.sem = sem
        self.count = 0


class Prog:
    ENGS = ("pe", "act", "dve", "pool", "sp")

    def __init__(self, nc, es):
        self.nc = nc
        self.es = es
        self.sem = {e: es.enter_context(nc.semaphore("s_" + e)) for e in self.ENGS}
        self.cnt = {e: 0 for e in self.ENGS}
        self.seen = {e: {} for e in self.ENGS}
        self.q = {e: [] for e in self.ENGS}
        self.bar = {e: {} for e in self.ENGS}
        self.dsem = {}
        self.ninstr = 0

    def semh(self, k):
        if isinstance(k, tuple):
            return self.dsem[k[1]].sem
        return self.sem[k]

    def _deps(self, eng, reads, writes, skip_same=True, own=None):
        need = dict(self.bar[eng])
        self.bar[eng] = {}

        def add(k, v):
            if skip_same and k == eng:
                return
            if own is not None and k == own:
                return
            if need.get(k, 0) < v:
                need[k] = v

        for r in reads:
            if r.w is not None:
                add(*r.w)
            if r.excl:
                for k, v in r.r.items():
                    if k != eng:
                        add(k, v)
        for w in writes:
            if w.w is not None:
                add(*w.w)
            for k, v in w.r.items():
                add(k, v)
        waits = []
        seen = self.seen[eng]
        for k, v in need.items():
            if skip_same and k == eng:
                continue
            if seen.get(k, 0) < v:
                seen[k] = v
                waits.append((k, v))
        return waits

    def op(self, eng, fn, reads=(), writes=()):
        waits = self._deps(eng, reads, writes, skip_same=(eng == "pe"))
        self.cnt[eng] += 1
        v = self.cnt[eng]
        self.q[eng].append((waits, fn, (eng, 1)))
        for r in reads:
            if r.r.get(eng, 0) < v:
                r.r[eng] = v
        for w in writes:
            w.w = (eng, v)
            w.r = {}
        self.ninstr += 1

    def dma(self, queue, key, fn, reads=(), writes=(), part=False):
        if key not in self.dsem:
            self.dsem[key] = DSem(self.es.enter_context(self.nc.semaphore("d_" + key)))
        ds = self.dsem[key]
        k = ("d", key)
        waits = self._deps(queue, reads, writes, skip_same=False, own=k)
        if not part and ds.count > 0 and self.seen[queue].get(k, 0) < ds.count:
            self.seen[queue][k] = ds.count
            waits.append((k, ds.count))
        ds.count += 16
        self.q[queue].append((waits, fn, (k, 16)))
        for r in reads:
            if r.r.get(k, 0) < ds.count:
                r.r[k] = ds.count
        for w in writes:
            w.w = (k, ds.count)
            w.r = {}
        self.ninstr += 1

    def barrier(self):
        toks = {e: self.cnt[e] for e in ("pe", "act", "dve", "pool") if self.cnt[e] > 0}
        for k, ds in self.dsem.items():
            if ds.count > 0:
                toks[("d", k)] = ds.count
        for e in self.ENGS:
            b = self.bar[e]
            for k, v in toks.items():
                if b.get(k, 0) < v:
                    b[k] = v

    def emit(self):
        nc = self.nc
        handles = {"pe": "tensor", "act": "scalar", "dve": "vector", "pool": "gpsimd", "sp": "sync"}
        final = []
        for k, ds in self.dsem.items():
            if ds.count > 0:
                final.append((("d", k), ds.count))
        for e in ("pe", "act", "dve", "pool"):
            if self.cnt[e] > 0:
                final.append((e, self.cnt[e]))
        with nc.Block() as block:
            for name in self.ENGS:
                lst = self.q[name]

                def body(e, lst=lst, name=name):
                    for waits, fn, inc in lst:
                        for (k, v) in waits:
                            e.wait_ge(self.semh(k), v)
                        ins = fn(e)
                        ins.then_inc(self.semh(inc[0]), inc[1])
                    if name == "sp":
                        for (k, v) in final:
                            e.wait_ge(self.semh(k), v)

                getattr(block, handles[name])(body)


def build_nc():
    nc = bass.Bass("TRN2", target_bir_lowering=False)

    def din(name, shape, dt=F32):
        return nc.dram_tensor(name, list(shape), dt, kind="ExternalInput").ap()

    def dout(name, shape):
        return nc.dram_tensor(name, list(shape), F32, kind="ExternalOutput").ap()

    xs_d = din("xs", [TS, D])
    xp_d = din("xp", [512, D])
    cdk_d = din("cdk", [2, 256, 512])
    cdv_d = din("cdv", [2, 256, 512])
    cwk_d = din("cwk", [2, 256, 128])
    cwv_d = din("cwv", [2, 256, 128])
    condT_d = din("condT", [128, 16])
    wmod_d = din("w_mod", [2, D, 9 * D])
    bmodT_d = din("bmodT", [128, 144])
    normgT_d = din("normgT", [128, 48])
    ffnwi_d = din("ffn_w_in", [2, 2, D, 2 * DFF])
    ffnwo_d = din("ffn_w_out", [2, 2, DFF, D])
    win_d = din("w_in", [2, D, DIN])
    lam_d = din("lamb", [128, 512])
    subg_d = din("subgT", [128, 2])
    sink_d = din("sinkT", [128, 8])
    convw_d = din("convwT", [128, 2 * 4 * CONVK])
    convp_d = din("convpT", [128, 24])
    wbr_d = din("w_branch", [2, 3, 512, D])
    wout_d = din("w_out", [2, D, D])
    fing_d = din("fingT", [128, 8])
    cos_d = din("cosT", [128, TS])
    sin_d = din("sinT", [128, TS])
    cst_d = din("cst", [128, 128 * 3 + 384])

    ys_d = dout("ys", [TS, D])
    yp_d = dout("yp", [512, D])
    ndk_d = dout("ndk", [2, 2, 256, 512])
    ndv_d = dout("ndv", [2, 2, 256, 512])
    nwk_d = dout("nwk", [2, 2, 256, 128])
    nwv_d = dout("nwv", [2, 2, 256, 128])

    es = ExitStack()
    with es:
        arena = es.enter_context(nc.sbuf_tensor("arena", [128, ARENA_BYTES // 4], F32))
        arena_b = arena.bitcast(BF16)
        banks = [es.enter_context(nc.psum_tensor(f"bank{i}", [128, 512], F32)) for i in range(8)]
        P = Prog(nc, es)

        def vf(off, n):
            assert off % 4 == 0
            return arena[:, off // 4: off // 4 + n]

        def vb(off, n):
            assert off % 2 == 0
            return arena_b[:, off // 2: off // 2 + n]

        def r3(ap, a):
            return ap.rearrange("p (a b) -> p a b", a=a)

        X = r3(vf(OFF_X, KC * TOK), KC)
        N = r3(vb(OFF_N, KC * TOK), KC)
        Xr = [Res(f"X{t}") for t in range(NT)]
        Nr = [Res(f"N{t}") for t in range(NT)]
        co = [OFF_C]

        def calloc(nbytes):
            o = co[0]
            co[0] += (nbytes + 31) // 32 * 32
            assert co[0] <= OFF_RING
            return o

        ident_f = vf(calloc(512), 128)
        rot_f = vf(calloc(512), 128)
        ones_f = vf(calloc(512), 128)
        ones_b = vb(calloc(256), 128)
        ident_b = vb(calloc(256), 128)
        mask_b = vb(calloc(768), 384)
        MOD = vf(calloc(1152), 288).rearrange("p (l v k c) -> p l v k c", l=2, v=9, k=8)
        DER = vf(calloc(640), 160).rearrange("p (l v k c) -> p l v k c", l=2, v=5, k=8)
        normg = vf(calloc(192), 48).rearrange("p (l i k) -> p l i k", l=2, i=3)
        fing = vf(calloc(32), 8)
        subg = vf(calloc(8), 2)
        gsub = vf(calloc(8), 2)
        neglam = vf(calloc(8), 2)
        sinkexp = vf(calloc(32), 8).rearrange("p (l j) -> p l j", l=2)
        convw = vf(calloc(992), 248).rearrange("p (l c k) -> p l c k", l=2, c=4)
        convp = vf(calloc(96), 24).rearrange("p (l t c) -> p l t c", l=2, t=3)
        bmod = vf(calloc(576), 144).rearrange("p (l j) -> p l j", l=2)
        scT = vf(calloc(64), 16).rearrange("p (k c) -> p k c", k=8)
        lamtmp = vf(calloc(32), 8)
        Cr = Res("consts")

        NTF, NTB = 8, 4
        tf_ap = [vf(OFF_RING + i * 2048, 512) for i in range(NTF)]
        tf_res = [Res(f"tf{i}") for i in range(NTF)]
        tb_ap = [vb(OFF_RING + NTF * 2048 + i * 1024, 512) for i in range(NTB)]
        tb_res = [Res(f"tb{i}") for i in range(NTB)]
        ring_i = {"f": 0, "b": 0, "nf": NTF}

        def tmpf():
            i = ring_i["f"] % ring_i["nf"]
            ring_i["f"] = (i + 1) % ring_i["nf"]
            return tf_ap[i], tf_res[i]

        def tmpb():
            i = ring_i["b"]
            ring_i["b"] = (i + 1) % NTB
            return tb_ap[i], tb_res[i]

        bank_res = [Res(f"bank{i}", excl=True) for i in range(8)]
        bank_i = {"rot": 0, "acc": 0}
        ROT = (0, 1, 2, 3)
        acc_pool = {"lst": (4, 5, 6, 7)}

        def bank(pool="rot"):
            lst = ROT if pool == "rot" else acc_pool["lst"]
            i = lst[bank_i[pool] % len(lst)]
            bank_i[pool] += 1
            return banks[i][:, :], bank_res[i]

        lo = [OFF_L]

        def lreset(keep=0):
            lo[0] = OFF_L + keep

        def lalloc(nbytes):
            o = lo[0]
            lo[0] += (nbytes + 31) // 32 * 32
            assert lo[0] <= ARENA_BYTES, (lo[0], ARENA_BYTES)
            return o

        def tile_cols(tt):
            return slice(tt * 512, (tt + 1) * 512)

        def mm(out, lhsT, rhs, start, stop, reads, wres):
            P.op("pe", lambda e: e.matmul(out, lhsT=lhsT, rhs=rhs, start=start, stop=stop),
                 reads=reads, writes=[wres])

        def act(out, in_, func, reads, writes, bias=None, scale=None):
            kw = {}
            if bias is not None:
                kw["bias"] = bias
            if scale is not None:
                kw["scale"] = scale
            P.op("act", lambda e: e.activation(out=out, in_=in_, func=func, **kw), reads=reads, writes=writes)

        def tt_op(eng, out, in0, in1, op, reads, writes):
            P.op(eng, lambda e: e.tensor_tensor(out=out, in0=in0, in1=in1, op=op), reads=reads, writes=writes)

        def stt(out, in0, scalar, in1, op0, op1, reads, writes):
            P.op("dve", lambda e: e.scalar_tensor_tensor(out=out, in0=in0, scalar=scalar, in1=in1, op0=op0, op1=op1),
                 reads=reads, writes=writes)

        def ts_op(eng, out, in0, s1, s2, op0, op1, reads, writes):
            if op1 is None:
                P.op(eng, lambda e: e.tensor_scalar(out=out, in0=in0, scalar1=s1, scalar2=None, op0=op0),
                     reads=reads, writes=writes)
            else:
                P.op(eng, lambda e: e.tensor_scalar(out=out, in0=in0, scalar1=s1, scalar2=s2, op0=op0, op1=op1),
                     reads=reads, writes=writes)

        def copy(eng, out, in_, reads, writes):
            if eng == "act":
                P.op("act", lambda e: e.copy(out=out, in_=in_), reads=reads, writes=writes)
            else:
                P.op(eng, lambda e: e.tensor_copy(out=out, in_=in_), reads=reads, writes=writes)

        def rsqrt_bank(bk, bkr, scale, eps):
            t1, t1r = tmpf()
            act(t1, bk, AF.Ln, [bkr], [t1r], bias=eps, scale=scale)
            t2, t2r = tmpf()
            act(t2, t1, AF.Exp, [t1r], [t2r], scale=-0.5)
            return t2, t2r

        def ld(key, out, in_, wres, queue="sp"):
            P.dma(queue, key, lambda e: e.dma_start(out=out, in_=in_), reads=[], writes=[wres], part=True)

        cst_t = vf(lalloc(768 * 4), 768)
        ld("c0", cst_t, cst_d, Cr)
        ld("c0", bmod.rearrange("p l j -> p (l j)"), bmodT_d, Cr)
        ld("c0", normg.rearrange("p l i k -> p (l i k)"), normgT_d, Cr)
        ld("c0", fing, fing_d, Cr)
        ld("c0", subg, subg_d, Cr)
        ld("c0", sinkexp.rearrange("p l j -> p (l j)"), sink_d, Cr)
        ld("c0", convw.rearrange("p l c k -> p (l c k)"), convw_d, Cr)
        ld("c0", convp.rearrange("p l t c -> p (l t c)"), convp_d, Cr)
        ld("c0", scT.rearrange("p k c -> p (k c)"), condT_d, Cr)
        lam_t = vf(lalloc(2048), 512)
        ld("c0", lam_t, lam_d, Cr)
        copy("dve", ident_f, cst_t[:, 0:128], [Cr], [Cr])
        copy("dve", rot_f, cst_t[:, 128:256], [Cr], [Cr])
        copy("dve", ones_f, cst_t[:, 256:384], [Cr], [Cr])
        copy("dve", ones_b, cst_t[:, 256:384], [Cr], [Cr])
        copy("dve", ident_b, cst_t[:, 0:128], [Cr], [Cr])
        copy("dve", mask_b, cst_t[:, 384:768], [Cr], [Cr])
        act(scT.rearrange("p k c -> p (k c)"), scT.rearrange("p k c -> p (k c)"), AF.Silu, [Cr], [Cr])
        act(sinkexp.rearrange("p l j -> p (l j)"), sinkexp.rearrange("p l j -> p (l j)"), AF.Exp, [Cr], [Cr])
        for l in range(2):
            lam_init = 0.8 - 0.6 * math.exp(-0.3 * l)
            base = l * 256
            pr = vf(lalloc(1024), 256)
            tt_op("dve", pr[:, 0:64], lam_t[:, base:base + 64], lam_t[:, base + 64:base + 128], ALU.mult, [Cr], [Cr])
            tt_op("dve", pr[:, 64:128], lam_t[:, base + 128:base + 192], lam_t[:, base + 192:base + 256], ALU.mult, [Cr], [Cr])
            P.op("dve", lambda e, pr=pr: e.reduce_sum(out=lamtmp[:, 0:1], in_=pr[:, 0:64], axis=mybir.AxisListType.X),
                 reads=[Cr], writes=[Cr])
            P.op("dve", lambda e, pr=pr: e.reduce_sum(out=lamtmp[:, 1:2], in_=pr[:, 64:128], axis=mybir.AxisListType.X),
                 reads=[Cr], writes=[Cr])
            act(lamtmp[:, 2:4], lamtmp[:, 0:2], AF.Exp, [Cr], [Cr])
            tt_op("dve", lamtmp[:, 4:5], lamtmp[:, 3:4], lamtmp[:, 2:3], ALU.subtract, [Cr], [Cr])
            ts_op("dve", neglam[:, l:l + 1], lamtmp[:, 4:5], -lam_init, None, ALU.add, None, [Cr], [Cr])
            ts_op("dve", gsub[:, l:l + 1], subg[:, l:l + 1], 1.0 - lam_init, None, ALU.mult, None, [Cr], [Cr])

        NWM = 3
        wm_slab = [r3(vb(lalloc(8192), 4096), 8) for _ in range(NWM)]
        wm_res = [Res(f"wm{i}") for i in range(NWM)]
        scTb = vb(calloc(32), 16).rearrange("p (k c) -> p k c", k=8)
        copy("dve", scTb, scT, [Cr], [Cr])

        def mod_slab(l, s, bk, bkr, sl, slr, key):
            P.dma("pool", key,
                  lambda e: e.dma_start(
                      out=sl, in_=wmod_d[l, :, s * 512:(s + 1) * 512].rearrange("(kc p) n -> p kc n", p=128)),
                  reads=[], writes=[slr])
            for fb in range(4):
                j = s * 4 + fb
                for kc in range(KC):
                    mm(bk[:, 2 * j:2 * j + 2], sl[:, kc, fb * 128:(fb + 1) * 128], scTb[:, kc, :],
                       kc == 0, kc == KC - 1, [slr, Cr], bkr)

        def mod_finish(l, bk, bkr, v_lo, v_hi):
            j0, j1 = v_lo * 8, v_hi * 8
            mod_l = MOD[:, l, v_lo:v_hi].rearrange("p v k c -> p (v k) c")
            bmv = bmod[:, l, j0:j1]
            bm_b = bass.AP(bmod.tensor, bmv.offset, [list(bmv.ap[0]), [1, j1 - j0], [0, 2]])
            tt_op("dve", mod_l, bk[:, 2 * j0:2 * j1].rearrange("p (j c) -> p j c", c=2), bm_b, ALU.add, [bkr, Cr], [Cr])
            for i in range(3):
                if v_lo <= 3 * i + 1 < v_hi:
                    g_b = bass.AP(normg.tensor, normg[:, l, i, :].offset, [list(normg[:, l, i, :].ap[0]), [1, 8], [0, 2]])
                    stt(DER[:, l, i], MOD[:, l, 3 * i + 1], 1.0, g_b, ALU.add, ALU.mult, [Cr], [Cr])
            if v_lo <= 2 < v_hi:
                ts_op("dve", DER[:, l, 3], MOD[:, l, 2], 0.5, None, ALU.mult, None, [Cr], [Cr])
            if v_lo <= 8 < v_hi:
                ts_op("dve", DER[:, l, 4], MOD[:, l, 8], 0.5, None, ALU.mult, None, [Cr], [Cr])

        bk0, bk0r = banks[7][:, :], bank_res[7]
        for s in range(6):
            mod_slab(0, s, bk0, bk0r, wm_slab[s % NWM], wm_res[s % NWM], f"wm{s % NWM}")
        mod_finish(0, bk0, bk0r, 0, 3)

        def make_mod_side_work(which):
            top = ARENA_BYTES - 2 * 8192
            sl2 = [r3(vb(top + i * 8192, 4096), 8) for i in range(2)]
            sl2r = [Res("wmB0"), Res("wmB1")]
            b7, b7r = banks[7][:, :], bank_res[7]
            work = []
            k = 0
            if which == 0:
                for s in range(6, 18):
                    work.append(lambda s=s, k=k: mod_slab(0, s, b7, b7r, sl2[k % 2], sl2r[k % 2], f"wmB{k % 2}"))
                    k += 1
                work.append(lambda: mod_finish(0, b7, b7r, 3, 9))
            else:
                for s in range(18):
                    work.append(lambda s=s, k=k: mod_slab(1, s, b7, b7r, sl2[k % 2], sl2r[k % 2], f"wmB{k % 2}"))
                    k += 1
                work.append(lambda: mod_finish(1, b7, b7r, 0, 9))
            return work

        def mcol(l, v, kc, cond):
            return MOD[:, l, v, kc, cond:cond + 1]

        def dcol(l, v, kc, cond):
            return DER[:, l, v, kc, cond:cond + 1]

        def norm_tile(l, i, tt):
            cond = 0 if tt < 4 else 1
            cs = tile_cols(tt)
            bk, bkr = bank("rot")
            for kc in range(KC):
                sqt, sqr_ = tmpf()
                sqb = sqt.bitcast(BF16)[:, 0:512]
                act(sqb, X[:, kc, cs], AF.Square, [Xr[tt]], [sqr_])
                mm(bk, ones_b, sqb, kc == 0, kc == KC - 1, [sqr_, Cr], bkr)
            rs, rsr = rsqrt_bank(bk, bkr, 1.0 / D, EPS)
            tl = [tmpf(), tmpf()]
            for kc in range(KC):
                t, tr = tl[kc % 2]
                tt_op("dve", t, X[:, kc, cs], rs, ALU.mult, [Xr[tt], rsr], [tr])
                act(N[:, kc, cs], t, AF.Identity, [tr, Cr], [Nr[tt]],
                    bias=mcol(l, 3 * i, kc, cond), scale=dcol(l, i, kc, cond))

        def norm_phase(l, i):
            for tt in range(NT):
                norm_tile(l, i, tt)

        xst = [vf(lalloc(4096), 1024) for _ in range(3)]
        xst_res = [Res(f"xst{i}") for i in range(3)]
        for tb in range(20):
            st, str_ = xst[tb % 3], xst_res[tb % 3]
            src = xs_d[tb * 128:(tb + 1) * 128, :] if tb < 16 else xp_d[(tb - 16) * 128:(tb - 15) * 128, :]
            P.dma("sp", f"xst{tb % 3}", lambda e, st=st, src=src: e.dma_start(out=st, in_=src), reads=[], writes=[str_])
            for g in range(2):
                bk, bkr = bank("rot")
                for j in range(4):
                    kc = 4 * g + j
                    P.op("pe", lambda e, bk=bk, j=j, st=st, kc=kc: e.transpose(
                        out=bk[:, j * 128:(j + 1) * 128], in_=st[:, kc * 128:(kc + 1) * 128], identity=ident_f),
                        reads=[str_, Cr], writes=[bkr])
                copy("act" if g == 0 else "dve", X[:, 4 * g:4 * g + 4, tb * 128:(tb + 1) * 128],
                     bk.rearrange("p (a b) -> p a b", a=4), [bkr], [Xr[tb // 4]])
            if tb % 4 == 3:
                norm_tile(0, 0, tb // 4)


        ffn_keep = {}

        final_state = {"done": False}

        def ffn_phase(l, f, fuse_norm=None, barrier=True, side_work=None, final_fuse=False):
            NS = 2 if final_fuse else 3
            if barrier or not ffn_keep:
                P.barrier()
                ffn_keep["wir"] = [Res(f"wi{i}") for i in range(NS)]
                ffn_keep["wor"] = [Res(f"wo{i}") for i in range(NS)]
                ffn_keep["hr"] = [Res("h0"), Res("h1")]
            lreset()
            wi = [r3(vb(lalloc(8192), 4096), 8) for _ in range(NS)]
            wo = [r3(vb(lalloc(4096), 2048), 2) for _ in range(NS)]
            wir = ffn_keep["wir"]
            wor = ffn_keep["wor"]
            hb = [r3(vb(lalloc(2048), 1024), 2) for _ in range(2)]
            hr = ffn_keep["hr"]
            gv = 3 if f == 0 else 4
            hi = 0
            fin_pend = []
            if final_fuse:
                xn = r3(vf(lalloc(16384), 4096), 8)
                xnr = Res("xnF")
                ostF = [vf(lalloc(4096), 1024) for _ in range(2)]
                ostFr = [Res("ostF0"), Res("ostF1")]
                foi = [0]
                final_state["done"] = True

                def final_part_a(tt):
                    cs = tile_cols(tt)
                    bk, bkr = bank("rot")
                    for kc in range(KC):
                        sqt, sqr_ = tmpf()
                        sqb = sqt.bitcast(BF16)[:, 0:512]
                        act(sqb, X[:, kc, cs], AF.Square, [Xr[tt]], [sqr_])
                        mm(bk, ones_b, sqb, kc == 0, kc == KC - 1, [sqr_, Cr], bkr)
                    rs, rsr = rsqrt_bank(bk, bkr, 1.0 / D, EPS)
                    for kc in range(KC):
                        stt(xn[:, kc, :], X[:, kc, cs], fing[:, kc:kc + 1], rs, ALU.mult, ALU.mult,
                            [Xr[tt], rsr, Cr], [xnr])

                def final_part_b(tt):
                    for b in range(4):
                        o, orr = ostF[foi[0] % 2], ostFr[foi[0] % 2]
                        key = f"foF{foi[0] % 2}"
                        foi[0] += 1
                        for g2 in range(2):
                            bk, bkr = bank("rot")
                            for j in range(4):
                                kc = 4 * g2 + j
                                P.op("pe", lambda e, bk=bk, j=j, kc=kc, b=b: e.transpose(
                                    out=bk[:, j * 128:(j + 1) * 128], in_=xn[:, kc, b * 128:(b + 1) * 128],
                                    identity=ident_f), reads=[xnr, Cr], writes=[bkr])
                            copy("act" if g2 == 0 else "dve", o[:, g2 * 512:(g2 + 1) * 512], bk, [bkr], [orr])
                        tb = tt * 4 + b
                        dst = ys_d[tb * 128:(tb + 1) * 128, :] if tb < 16 else yp_d[(tb - 16) * 128:(tb - 15) * 128, :]
                        P.dma("sp", key, lambda e, o=o, dst=dst: e.dma_start(out=dst, in_=o), reads=[orr], writes=[])
            if side_work:
                acc_pool["lst"] = (4, 5, 6)

            def load_group(g):
                s = g % NS
                w_i, w_o = wi[s], wo[s]
                P.dma("pool", f"wi{s}", lambda e: e.dma_start(
                    out=w_i[:, :, 0:256], in_=ffnwi_d[l, f, :, 256 * g:256 * g + 256].rearrange("(kc p) n -> p kc n", p=128)),
                    reads=[], writes=[wir[s]])
                P.dma("pool", f"wi{s}", lambda e: e.dma_start(
                    out=w_i[:, :, 256:512],
                    in_=ffnwi_d[l, f, :, DFF + 256 * g:DFF + 256 * g + 256].rearrange("(kc p) n -> p kc n", p=128)),
                    reads=[], writes=[wir[s]], part=True)
                P.dma("pool", f"wo{s}", lambda e: e.dma_start(
                    out=w_o, in_=ffnwo_d[l, f, 256 * g:256 * g + 256, :].rearrange("(j p) n -> p j n", p=128)),
                    reads=[], writes=[wor[s]])

            def second_stage(s, tt, h, hrr):
                cond = 0 if tt < 4 else 1
                cs = tile_cols(tt)
                w_o = wo[s]
                for fo in range(KC):
                    bo, bor = bank("acc")
                    for j in range(2):
                        mm(bo, w_o[:, j, fo * 128:(fo + 1) * 128], h[:, j, :], j == 0, j == 1, [wor[s], hrr], bor)
                    stt(X[:, fo, cs], bo, dcol(l, gv, fo, cond), X[:, fo, cs], ALU.mult, ALU.add,
                        [bor, Cr, Xr[tt]], [Xr[tt]])

            prev = None
            load_group(0)
            for g in range(NG):
                s = g % NS
                w_i = wi[s]
                if g + 1 < NG and NS == 3:
                    load_group(g + 1)
                if g == NG - 1:
                    while side_work:
                        side_work.pop(0)()
                for tt in range(NT):
                    if side_work and tt in (1, 2, 3):
                        side_work.pop(0)()
                    cs = tile_cols(tt)
                    h, hrr = hb[hi % 2], hr[hi % 2]
                    hi += 1
                    for j in range(2):
                        ba, bar_ = bank("rot")
                        for kc in range(KC):
                            mm(ba, w_i[:, kc, j * 128:(j + 1) * 128], N[:, kc, cs], kc == 0, kc == KC - 1,
                               [wir[s], Nr[tt]], bar_)
                        bb, bbr = bank("rot")
                        for kc in range(KC):
                            mm(bb, w_i[:, kc, 256 + j * 128:256 + (j + 1) * 128], N[:, kc, cs], kc == 0, kc == KC - 1,
                               [wir[s], Nr[tt]], bbr)
                        sg, sgr = tmpf()
                        act(sg, ba, AF.Silu, [bar_], [sgr])
                        tt_op("dve", h[:, j, :], sg, bb, ALU.mult, [sgr, bbr], [hrr])
                    while fin_pend:
                        fin_pend.pop(0)()
                    if prev is not None:
                        second_stage(*prev)
                        if fuse_norm is not None and g == NG - 1 and tt >= 1:
                            norm_tile(fuse_norm[0], fuse_norm[1], prev[1])
                        if final_fuse and g == NG - 1 and tt >= 1:
                            final_part_a(prev[1])
                            fin_pend.append(lambda t_=prev[1]: final_part_b(t_))
                    if NS == 2 and tt == 0 and g + 1 < NG:
                        load_group(g + 1)
                    prev = (s, tt, h, hrr)
            second_stage(*prev)
            while fin_pend:
                fin_pend.pop(0)()
            if final_fuse:
                final_part_a(prev[1])
                final_part_b(prev[1])
            while side_work:
                side_work.pop(0)()
            acc_pool["lst"] = (4, 5, 6, 7)
            if fuse_norm is not None:
                norm_tile(fuse_norm[0], fuse_norm[1], prev[1])

        Y_BYTES = 4 * TOK * 2

        def rope_store(bk, bkr, dst, dres, tabs, eng2="dve", split=None):
            cosT, sinT, tabr = tabs
            qf, qfr = tmpf()
            copy("act", qf, bk, [bkr], [qfr])
            br, brr = bank("rot")
            mm(br, rot_f, qf, True, True, [qfr, Cr], brr)
            t1, t1r = tmpf()
            tt_op(eng2, t1, qf, cosT, ALU.mult, [qfr, tabr], [t1r])
            t2, t2r = tmpf()
            tt_op("dve", t2, br, sinT, ALU.mult, [brr, tabr], [t2r])
            if split is None:
                tt_op(eng2, dst, t1, t2, ALU.add, [t1r, t2r], [dres])
            else:
                d0, d1 = split
                tt_op(eng2, d0[0:64, :], t1[0:64, :], t2[0:64, :], ALU.add, [t1r, t2r], [dres])
                tt_op(eng2, d1[64:128, :], t1[64:128, :], t2[64:128, :], ALU.add, [t1r, t2r], [dres])

        def load_tabs(tabbuf, tabres, idx, tt):
            cosT, sinT = tabbuf[idx % 2]
            tr = tabres[idx % 2]
            P.dma("sp", f"tab{idx % 2}", lambda e: e.dma_start(out=cosT, in_=cos_d[:, tt * 512:(tt + 1) * 512]),
                  reads=[], writes=[tr])
            P.dma("sp", f"tab{idx % 2}", lambda e: e.dma_start(out=sinT, in_=sin_d[:, tt * 512:(tt + 1) * 512]),
                  reads=[], writes=[tr], part=True)
            return cosT, sinT, tr

        ost_i = [0]

        def store_rows(src_ap, src_res, dst_ap):
            k = f"ost{ost_i[0] % 4}"
            ost_i[0] += 1
            P.dma("sp", k, lambda e: e.dma_start(out=dst_ap, in_=src_ap), reads=[src_res], writes=[])

        def consume(l, bi, Y, Yr, row_perm=None, fuse_norm=None):
            P.barrier()
            lreset(Y_BYTES)
            wg = r3(vb(lalloc(16384), 8192), 8)
            wb_ = r3(vb(lalloc(8192), 4096), 4)
            wo = r3(vb(lalloc(16384), 8192), 8)
            m = r3(vb(OFF_RING + (NTF - 2) * 2048, 4096), 8)
            ring_i["nf"] = NTF - 2
            wor, mr = Res("wo"), Res("m")
            wgr = [Res(f"wg{i}") for i in range(KC)]
            wbr = [Res("wbA"), Res("wbB")]
            c0 = C_G + bi * D

            def load_wg(ob):
                P.dma("pool", f"cwg{ob}", lambda e: e.dma_start(
                    out=wg[:, :, ob * 128:(ob + 1) * 128],
                    in_=win_d[l, :, c0 + ob * 128:c0 + (ob + 1) * 128].rearrange("(kc p) n -> p kc n", p=128)),
                    reads=[], writes=[wgr[ob]])

            def load_wb(hf):
                cs_ = slice(hf * 512, (hf + 1) * 512)
                if row_perm is None:
                    P.dma("pool", f"cwb{hf}", lambda e: e.dma_start(
                        out=wb_[:, :, cs_], in_=wbr_d[l, bi, :, cs_].rearrange("(kc p) n -> p kc n", p=128)),
                        reads=[], writes=[wbr[hf]])
                else:
                    for j in range(4):
                        for half in range(2):
                            r0 = row_perm(j, half)
                            P.dma("pool", f"cwb{hf}", lambda e, j=j, half=half, r0=r0: e.dma_start(
                                out=wb_[half * 64:(half + 1) * 64, j, cs_], in_=wbr_d[l, bi, r0:r0 + 64, cs_]),
                                reads=[], writes=[wbr[hf]], part=(j + half > 0))

            load_wg(0)
            load_wb(0)
            for ob in range(1, 4):
                load_wg(ob)
            load_wb(1)
            for ob in range(4, KC):
                load_wg(ob)
            P.dma("pool", "cwo", lambda e: e.dma_start(
                out=wo, in_=wout_d[l].rearrange("(kc p) n -> p kc n", p=128)), reads=[], writes=[wor])
            for tt in range(NT):
                cond = 0 if tt < 4 else 1
                cs = tile_cols(tt)
                for ob in range(KC):
                    bg, bgr = bank("rot")
                    for kc in range(KC):
                        mm(bg, wg[:, kc, ob * 128:(ob + 1) * 128], N[:, kc, cs], kc == 0, kc == KC - 1,
                           [wgr[ob], Nr[tt]], bgr)
                    bp, bpr = bank("rot")
                    for kc in range(4):
                        mm(bp, wb_[:, kc, ob * 128:(ob + 1) * 128], Y[:, kc, cs], kc == 0, kc == 3,
                           [wbr[ob // 4], Yr[tt]], bpr)
                    sg, sgr = tmpf()
                    act(sg, bg, AF.Sigmoid, [bgr], [sgr])
                    tt_op("dve", m[:, ob, :], sg, bp, ALU.mult, [sgr, bpr], [mr])
                    if fuse_norm is not None and tt >= 1 and ob == 3:
                        norm_tile(fuse_norm[0], fuse_norm[1], tt - 1)
                for fo in range(KC):
                    bo, bor = bank("acc")
                    for ob in range(KC):
                        mm(bo, wo[:, ob, fo * 128:(fo + 1) * 128], m[:, ob, :], ob == 0, ob == KC - 1,
                           [wor, mr], bor)
                    stt(X[:, fo, cs], bo, mcol(l, 5, fo, cond), X[:, fo, cs], ALU.mult, ALU.add,
                        [bor, Cr, Xr[tt]], [Xr[tt]])
            if fuse_norm is not None:
                norm_tile(fuse_norm[0], fuse_norm[1], NT - 1)
            P.barrier()
            ring_i["nf"] = NTF
            ring_i["f"] = 0

        def stage_A(l):
            P.barrier()
            lreset()
            Y = r3(vb(lalloc(Y_BYTES), 4 * TOK), 4)
            Yr = [Res(f"Y{t}") for t in range(NT)]
            QTp = [vb(lalloc(4096), 2048) for _ in range(2)]
            KT = vb(lalloc(4608), 2304)
            V = r3(vb(lalloc(4608), 2304), 18)
            QTr, KTr, Vr = Res("QT"), Res("KT"), Res("V")
            P.op("pool", lambda e: e.memset(QTp[0][64:128, :], 0.0), reads=[], writes=[QTr])
            P.op("pool", lambda e: e.memset(QTp[1][0:64, :], 0.0), reads=[], writes=[QTr])
            wh = [r3(vb(lalloc(6144), 3072), 8) for _ in range(2)]
            whr = [Res("wh0"), Res("wh1")]
            tabbuf = [(vf(lalloc(2048), 512), vf(lalloc(2048), 512)) for _ in range(2)]
            tabres = [Res("tab0"), Res("tab1")]
            tab_i = 0

            pend = []

            def flush():
                while pend:
                    pend.pop(0)()

            LOOK = 3

            def attn(qcols, kchunks, ycols, h):
                nq = qcols.stop - qcols.start
                acc = [(bank("acc"), bank("acc")) for _ in range(2)]
                items = [(c, kc, idx) for idx, kc in enumerate(kchunks) for c in range(2)]
                n = len(items)
                nk = len(kchunks)
                es_ = {}

                def s_stage(i):
                    c, kc, idx = items[i]
                    ps = slice(c * 64, (c + 1) * 64)
                    bs, bsr = bank("rot")
                    mm(bs[:, 0:nq], KT[:, kc * 128:(kc + 1) * 128], QTp[c][:, qcols], True, True, [KTr, QTr], bsr)
                    e_, er = tmpb()
                    act(e_[:, 0:nq], bs[:, 0:nq], AF.Exp, [bsr], [er], scale=SCALE)
                    es_[i] = (e_, er)

                ZG = 9
                ngrp = (nk + ZG - 1) // ZG
                zst = {}
                zpend = []

                def av_stage(i):
                    c, kc, idx = items[i]
                    (bu, bur), (bz, bzr) = acc[c]
                    e_, er = es_.pop(i)
                    while zpend:
                        zpend.pop(0)()
                    mm(bu[:, 0:nq], V[:, kc, :], e_[:, 0:nq], idx == 0, idx == nk - 1, [Vr, er], bur)
                    g, pos = divmod(idx, ZG)
                    gsize = min(ZG, nk - g * ZG)
                    if gsize == 1:
                        mm(bz[:, 0:nq], ones_b, e_[:, 0:nq], g == 0, g == ngrp - 1, [Cr, er], bzr)
                        return
                    if pos == 0:
                        t, tr = tmpf()
                        zb = t.bitcast(BF16)[:, 0:512]
                        zst[c] = (zb, tr)
                        copy("dve", zb[:, 0:nq], e_[:, 0:nq], [er], [tr])
                    else:
                        zb, tr = zst[c]
                        tt_op("dve", zb[:, 0:nq], zb[:, 0:nq], e_[:, 0:nq], ALU.add, [tr, er], [tr])
                    if pos == gsize - 1:
                        zpend.append(lambda zb=zb, tr=tr, g=g, bz=bz, bzr=bzr: mm(
                            bz[:, 0:nq], ones_b, zb[:, 0:nq], g == 0, g == ngrp - 1, [Cr, tr], bzr))

                for i in range(n + LOOK):
                    if i < n:
                        s_stage(i)
                    if i == LOOK - 1:
                        flush()
                    if i >= LOOK:
                        av_stage(i - LOOK)
                while zpend:
                    zpend.pop(0)()
                ts_ = []
                ev = []
                for c in range(2):
                    (bu, bur), (bz, bzr) = acc[c]
                    uc, ucr = tmpf()
                    copy("dve", uc[:, 0:nq], bu[:, 0:nq], [bur], [ucr])
                    zc, zcr = tmpf()
                    copy("dve", zc[:, 0:nq], bz[:, 0:nq], [bzr], [zcr])
                    ev.append((uc, ucr, zc, zcr))
                for c in range(2):
                    uc, ucr, zc, zcr = ev[c]
                    act(zc[:, 0:nq], zc[:, 0:nq], AF.Ln, [zcr], [zcr])
                    act(zc[:, 0:nq], zc[:, 0:nq], AF.Exp, [zcr], [zcr], scale=-1.0)
                    tt_op("dve", uc[:, 0:nq], uc[:, 0:nq], zc[:, 0:nq], ALU.mult, [ucr, zcr], [ucr])
                    ts_.append((uc, ucr))
                o, orr = tmpf()
                stt(o[:, 0:nq], ts_[1][0][:, 0:nq], neglam[:, l:l + 1], ts_[0][0][:, 0:nq], ALU.mult, ALU.add,
                    [ts_[0][1], ts_[1][1], Cr], [orr])
                sq, sqr_ = tmpb()
                act(sq[:, 0:nq], o[:, 0:nq], AF.Square, [orr], [sqr_])
                t2, t2r = tmpf()

                def part2():
                    bs, bsr = bank("rot")
                    mm(bs[:, 0:nq], ones_b, sq[:, 0:nq], True, True, [Cr, sqr_], bsr)
                    act(t2[:, 0:nq], bs[:, 0:nq], AF.Ln, [bsr], [t2r], bias=SUBEPS, scale=1.0 / 128)
                    act(t2[:, 0:nq], t2[:, 0:nq], AF.Exp, [t2r], [t2r], scale=-0.5)
                    stt(Y[:, h, ycols], o[:, 0:nq], gsub[:, l:l + 1], t2[:, 0:nq], ALU.mult, ALU.mult,
                        [orr, t2r, Cr], [Yr[ycols.start // 512]])

                pend.append(part2)

            for h in range(4):
                w_, wr = wh[h % 2], whr[h % 2]
                for part, c0 in enumerate((C_QA, C_KA, C_VA)):
                    P.dma("pool", f"wh{h % 2}", lambda e, w_=w_, part=part, c0=c0, h=h: e.dma_start(
                        out=w_[:, :, part * 128:(part + 1) * 128],
                        in_=win_d[l, :, c0 + h * 128:c0 + (h + 1) * 128].rearrange("(kc p) n -> p kc n", p=128)),
                        reads=[], writes=[wr], part=(part > 0))
                ck, ckr = tmpf()
                ck3 = ck[:, 0:256].rearrange("p (a b) -> p a b", a=2)
                P.dma("sp", "ck", lambda e, ck3=ck3, h=h: e.dma_start(
                    out=ck3, in_=cdk_d[l, :, h * 128:(h + 1) * 128].rearrange("(a p) n -> p a n", p=128)),
                    reads=[], writes=[ckr])
                bk, bkr = bank("rot")
                for a in range(2):
                    P.op("pe", lambda e, bk=bk, a=a, ck3=ck3: e.transpose(
                        out=bk[:, a * 128:(a + 1) * 128], in_=ck3[:, a, :], identity=ident_f),
                        reads=[ckr, Cr], writes=[bkr])
                copy("dve", KT[:, 0:256], bk[:, 0:256], [bkr], [KTr])
                P.dma("pool", "cv", lambda e, h=h: e.dma_start(
                    out=V[:, 0:2, :], in_=cdv_d[l, :, h * 128:(h + 1) * 128].rearrange("(a p) n -> p a n", p=128)),
                    reads=[], writes=[Vr])
                for tt in range(4):
                    cs = tile_cols(tt)
                    tabs = load_tabs(tabbuf, tabres, tab_i, tt)
                    tab_i += 1
                    bq, bqr = bank("rot")
                    for kc in range(KC):
                        mm(bq, w_[:, kc, 0:128], N[:, kc, cs], kc == 0, kc == KC - 1, [wr, Nr[tt]], bqr)
                    bk, bkr = bank("rot")
                    for kc in range(KC):
                        mm(bk, w_[:, kc, 128:256], N[:, kc, cs], kc == 0, kc == KC - 1, [wr, Nr[tt]], bkr)
                    bv, bvr = bank("rot")
                    for b in range(4):
                        for kc in range(KC):
                            mm(bv[:, b * 128:(b + 1) * 128], N[:, kc, tt * 512 + b * 128:tt * 512 + (b + 1) * 128],
                               w_[:, kc, 256:384], kc == 0, kc == KC - 1, [wr, Nr[tt]], bvr)
                    rope_store(bq, bqr, None, QTr, tabs, split=(QTp[0][:, cs], QTp[1][:, cs]))
                    rope_store(bk, bkr, KT[:, 256 + tt * 512:256 + (tt + 1) * 512], KTr, tabs)
                    copy("act", V[:, 2 + 4 * tt:6 + 4 * tt, :], bv.rearrange("p (a b) -> p a b", a=4), [bvr], [Vr])
                for qt in range(4):
                    attn(tile_cols(qt), list(range(18)), tile_cols(qt), h)
                flush()
                cs = tile_cols(4)
                bq, bqr = bank("rot")
                for kc in range(KC):
                    mm(bq, w_[:, kc, 0:128], N[:, kc, cs], kc == 0, kc == KC - 1, [wr, Nr[4]], bqr)
                copy("act", QTp[0][0:64, 0:512], bq[0:64, :], [bqr], [QTr])
                copy("act", QTp[1][64:128, 0:512], bq[64:128, :], [bqr], [QTr])
                bk, bkr = bank("rot")
                for kc in range(KC):
                    mm(bk, w_[:, kc, 128:256], N[:, kc, cs], kc == 0, kc == KC - 1, [wr, Nr[4]], bkr)
                copy("dve", KT[:, 0:512], bk, [bkr], [KTr])
                for part, dst in ((1, ndk_d), (2, ndv_d)):
                    bv, bvr = bank("rot")
                    for b in range(4):
                        for kc in range(KC):
                            mm(bv[:, b * 128:(b + 1) * 128], N[:, kc, 2048 + b * 128:2048 + (b + 1) * 128],
                               w_[:, kc, part * 128:(part + 1) * 128], kc == 0, kc == KC - 1, [wr, Nr[4]], bvr)
                    of, ofr = tmpf()
                    copy("act", of, bv, [bvr], [ofr])
                    if part == 2:
                        copy("dve", V[:, 0:4, :], bv.rearrange("p (a b) -> p a b", a=4), [bvr], [Vr])
                    for s in range(2):
                        store_rows(of.rearrange("p (a b) -> p a b", a=4)[:, 2 * s:2 * s + 2, :], ofr,
                                   dst[s, l, :, h * 128:(h + 1) * 128].rearrange("(a p) n -> p a n", p=128))
                for s in range(2):
                    attn(slice(s * 256, (s + 1) * 256), [2 * s, 2 * s + 1], slice(2048 + s * 256, 2048 + (s + 1) * 256), h)
                flush()
            return Y, Yr

        def stage_B(l):
            P.barrier()
            lreset()
            Y = r3(vb(lalloc(Y_BYTES), 4 * TOK), 4)
            Yr = [Res(f"Y{t}") for t in range(NT)]
            KT = vb(lalloc(4608), 2304)
            V = r3(vb(lalloc(4608), 2304), 18)
            KTr, Vr = Res("KTb"), Res("Vb")
            QTp = [vb(lalloc(4096), 2048) for _ in range(2)]
            QTpr = Res("QTb")
            P.op("pool", lambda e: e.memset(QTp[0][64:128, :], 0.0), reads=[], writes=[QTpr])
            P.op("pool", lambda e: e.memset(QTp[1][0:64, :], 0.0), reads=[], writes=[QTpr])
            wkv = r3(vb(lalloc(4096), 2048), 8)
            wkvr = Res("wkv")
            wq = [r3(vb(lalloc(2048), 1024), 8) for _ in range(2)]
            wqr = [Res("wq0"), Res("wq1")]
            tabbuf = [(vf(lalloc(2048), 512), vf(lalloc(2048), 512)) for _ in range(2)]
            tabres = [Res("tab0"), Res("tab1")]
            tab_i = 0
            P.dma("pool", "wkv", lambda e: e.dma_start(
                out=wkv, in_=win_d[l, :, C_KB:C_KB + 256].rearrange("(kc p) n -> p kc n", p=128)), reads=[], writes=[wkvr])
            ck, ckr = tmpf()
            ck3 = ck[:, 0:256].rearrange("p (a b) -> p a b", a=2)
            P.dma("sp", "ck", lambda e: e.dma_start(
                out=ck3, in_=cwk_d[l].rearrange("(a p) n -> p a n", p=128)), reads=[], writes=[ckr])
            bk, bkr = bank("rot")
            for a in range(2):
                P.op("pe", lambda e, bk=bk, a=a: e.transpose(
                    out=bk[:, a * 128:(a + 1) * 128], in_=ck3[:, a, :], identity=ident_f),
                    reads=[ckr, Cr], writes=[bkr])
            copy("dve", KT[:, 0:256], bk[:, 0:256], [bkr], [KTr])
            P.dma("pool", "cv", lambda e: e.dma_start(
                out=V[:, 0:2, :], in_=cwv_d[l].rearrange("(a p) n -> p a n", p=128)), reads=[], writes=[Vr])
            for tt in range(4):
                cs = tile_cols(tt)
                tabs = load_tabs(tabbuf, tabres, tab_i, tt)
                tab_i += 1
                bk, bkr = bank("rot")
                for kc in range(KC):
                    mm(bk, wkv[:, kc, 0:128], N[:, kc, cs], kc == 0, kc == KC - 1, [wkvr, Nr[tt]], bkr)
                bv, bvr = bank("rot")
                for b in range(4):
                    for kc in range(KC):
                        mm(bv[:, b * 128:(b + 1) * 128], N[:, kc, tt * 512 + b * 128:tt * 512 + (b + 1) * 128],
                           wkv[:, kc, 128:256], kc == 0, kc == KC - 1, [wkvr, Nr[tt]], bvr)
                rope_store(bk, bkr, KT[:, 256 + tt * 512:256 + (tt + 1) * 512], KTr, tabs)
                copy("act", V[:, 2 + 4 * tt:6 + 4 * tt, :], bv.rearrange("p (a b) -> p a b", a=4), [bvr], [Vr])

            LOOK = 3

            def attn_pair(j, qcols, ctx_chunks, local, ycols):
                nq = qcols.stop - qcols.start
                acc = [(bank("acc"), bank("acc")) for _ in range(2)]
                base = [(kc, 0, nq, None) for kc in ctx_chunks] + local
                nb = len(base)
                items = [(half, idx) for idx in range(nb) for half in range(2)]
                n = len(items)
                es_ = {}

                def s_stage(i):
                    half, idx = items[i]
                    kc, c_lo, c_hi, m_lo = base[idx]
                    w = c_hi - c_lo
                    bs, bsr = bank("rot")
                    mm(bs[:, 0:w], KT[:, kc * 128:(kc + 1) * 128], QTp[half][:, qcols.start + c_lo:qcols.start + c_hi],
                       True, True, [KTr, QTpr], bsr)
                    e_, er = tmpb()
                    act(e_[:, 0:w], bs[:, 0:w], AF.Exp, [bsr], [er], scale=SCALE)
                    if m_lo is not None:
                        tt_op("dve", e_[:, 0:w], e_[:, 0:w], mask_b[:, m_lo:m_lo + w], ALU.mult, [er, Cr], [er])
                    es_[i] = (e_, er)

                def av_stage(i):
                    half, idx = items[i]
                    kc, c_lo, c_hi, m_lo = base[idx]
                    w = c_hi - c_lo
                    (bu, bur), (bz, bzr) = acc[half]
                    e_, er = es_.pop(i)
                    mm(bu[:, c_lo:c_hi], V[:, kc, :], e_[:, 0:w], idx == 0, idx == nb - 1, [Vr, er], bur)
                    mm(bz[:, c_lo:c_hi], ones_b, e_[:, 0:w], idx == 0, idx == nb - 1, [Cr, er], bzr)

                for i in range(n + LOOK):
                    if i < n:
                        s_stage(i)
                    if i >= LOOK:
                        av_stage(i - LOOK)
                for half in range(2):
                    ps = slice(half * 64, (half + 1) * 64)
                    (bu, bur), (bz, bzr) = acc[half]
                    t1, t1r = tmpf()
                    act(t1[ps, 0:nq], bz[ps, 0:nq], AF.Ln, [bzr, Cr], [t1r], bias=sinkexp[ps, l, j:j + 1])
                    act(t1[ps, 0:nq], t1[ps, 0:nq], AF.Exp, [t1r], [t1r], scale=-1.0)
                    tt_op("dve", Y[ps, j, ycols], bu[ps, 0:nq], t1[ps, 0:nq], ALU.mult, [bur, t1r],
                          [Yr[ycols.start // 512]])

            for j in range(4):
                w_, wr = wq[j % 2], wqr[j % 2]
                for half in range(2):
                    hh = j + 4 * half
                    P.dma("pool", f"wq{j % 2}", lambda e, w_=w_, half=half, hh=hh: e.dma_start(
                        out=w_[:, :, half * 64:(half + 1) * 64],
                        in_=win_d[l, :, C_QB + hh * 64:C_QB + (hh + 1) * 64].rearrange("(kc p) n -> p kc n", p=128)),
                        reads=[], writes=[wr], part=(half > 0))
                for tt in range(4):
                    cs = tile_cols(tt)
                    tabs = load_tabs(tabbuf, tabres, tab_i, tt)
                    tab_i += 1
                    bq, bqr = bank("rot")
                    for kc in range(KC):
                        mm(bq, w_[:, kc, :], N[:, kc, cs], kc == 0, kc == KC - 1, [wr, Nr[tt]], bqr)
                    rope_store(bq, bqr, None, QTpr, tabs, split=(QTp[0][:, cs], QTp[1][:, cs]))
                for qt in range(4):
                    local = []
                    for jj in range(max(0, 4 * qt - 1), min(15, 4 * qt + 4) + 1):
                        qb_lo = max(4 * qt, jj - 1)
                        qb_hi = min(4 * qt + 3, jj + 1)
                        local.append((2 + jj, (qb_lo - 4 * qt) * 128, (qb_hi - 4 * qt + 1) * 128,
                                      (qb_lo - (jj - 1)) * 128))
                    attn_pair(j, tile_cols(qt), [0, 1], local, tile_cols(qt))
            cs = tile_cols(4)
            bk, bkr = bank("rot")
            for kc in range(KC):
                mm(bk, wkv[:, kc, 0:128], N[:, kc, cs], kc == 0, kc == KC - 1, [wkvr, Nr[4]], bkr)
            copy("dve", KT[:, 0:512], bk, [bkr], [KTr])
            for part, dst in ((0, nwk_d), (1, nwv_d)):
                bv, bvr = bank("rot")
                for b in range(4):
                    for kc in range(KC):
                        mm(bv[:, b * 128:(b + 1) * 128], N[:, kc, 2048 + b * 128:2048 + (b + 1) * 128],
                           wkv[:, kc, part * 128:(part + 1) * 128], kc == 0, kc == KC - 1, [wkvr, Nr[4]], bvr)
                of, ofr = tmpf()
                copy("act", of, bv, [bvr], [ofr])
                if part == 1:
                    copy("dve", V[:, 0:4, :], bv.rearrange("p (a b) -> p a b", a=4), [bvr], [Vr])
                for s in range(2):
                    store_rows(of.rearrange("p (a b) -> p a b", a=4)[:, 2 * s:2 * s + 2, :], ofr,
                               dst[s, l].rearrange("(a p) n -> p a n", p=128))
            for j in range(4):
                w_, wr = wq[j % 2], wqr[j % 2]
                for half in range(2):
                    hh = j + 4 * half
                    P.dma("pool", f"wq{j % 2}", lambda e, w_=w_, half=half, hh=hh: e.dma_start(
                        out=w_[:, :, half * 64:(half + 1) * 64],
                        in_=win_d[l, :, C_QB + hh * 64:C_QB + (hh + 1) * 64].rearrange("(kc p) n -> p kc n", p=128)),
                        reads=[], writes=[wr], part=(half > 0))
                bq, bqr = bank("rot")
                for kc in range(KC):
                    mm(bq, w_[:, kc, :], N[:, kc, cs], kc == 0, kc == KC - 1, [wr, Nr[4]], bqr)
                copy("act", QTp[0][0:64, 0:512], bq[0:64, :], [bqr], [QTpr])
                copy("act", QTp[1][64:128, 0:512], bq[64:128, :], [bqr], [QTpr])
                for s in range(2):
                    attn_pair(j, slice(s * 256, (s + 1) * 256), [2 * s, 2 * s + 1], [],
                              slice(2048 + s * 256, 2048 + (s + 1) * 256))
            return Y, Yr

        def stage_C(l):
            P.barrier()
            lreset()
            Y = r3(vb(lalloc(Y_BYTES), 4 * TOK), 4)
            Yr = [Res(f"Y{t}") for t in range(NT)]
            UW = 2656
            U = r3(vb(lalloc(4 * UW * 2), 4 * UW), 4)
            Ur = Res("U")
            mark = lo[0]
            wc = [r3(vb(lalloc(4096), 2048), 8) for _ in range(2)]
            wcr = [Res("wc0"), Res("wc1")]
            P.op("pool", lambda e: e.memset(U.rearrange("p a b -> p (a b)"), 0.0), reads=[], writes=[Ur])

            def ucol(tt):
                if tt < 4:
                    return [(HALO + tt * 512, 0, 512)]
                return [(2078 + HALO, 0, 256), (2078 + 286 + HALO, 256, 256)]

            for cc in range(4):
                w_, wr = wc[cc % 2], wcr[cc % 2]
                for part in range(2):
                    c0 = C_CU + part * 512 + cc * 128
                    P.dma("pool", f"wc{cc % 2}", lambda e, w_=w_, part=part, c0=c0: e.dma_start(
                        out=w_[:, :, part * 128:(part + 1) * 128],
                        in_=win_d[l, :, c0:c0 + 128].rearrange("(kc p) n -> p kc n", p=128)), reads=[], writes=[wr],
                        part=(part > 0))
                for tt in range(NT):
                    cs = tile_cols(tt)
                    ba, bar_ = bank("rot")
                    for kc in range(KC):
                        mm(ba, w_[:, kc, 0:128], N[:, kc, cs], kc == 0, kc == KC - 1, [wr, Nr[tt]], bar_)
                    bg, bgr = bank("rot")
                    for kc in range(KC):
                        mm(bg, w_[:, kc, 128:256], N[:, kc, cs], kc == 0, kc == KC - 1, [wr, Nr[tt]], bgr)
                    sg, sgr = tmpf()
                    act(sg, bg, AF.Sigmoid, [bgr], [sgr])
                    for (u0, t0, n) in ucol(tt):
                        tt_op("dve", U[:, cc, u0:u0 + n], ba[:, t0:t0 + n], sg[:, t0:t0 + n], ALU.mult,
                              [bar_, sgr], [Ur])
            P.barrier()
            lo[0] = mark
            DG = [r3(vb(lalloc(CONVK * 128 * 2), CONVK * 128), CONVK) for _ in range(2)]
            DGr = [Res("dg0"), Res("dg1")]
            di = 0
            for tt in range(NT):
                ys = []
                for cc in range(4):
                    dg, dgr = DG[di % 2], DGr[di % 2]
                    di += 1
                    idb = bass.AP(ident_b.tensor, ident_b.offset, [list(ident_b.ap[0]), [0, CONVK], [1, 128]])
                    wv = convw[:, l, cc, :]
                    wbc = bass.AP(wv.tensor, wv.offset, [list(wv.ap[0]), [1, CONVK], [0, 128]])
                    tt_op("pool", dg, idb, wbc, ALU.mult, [Cr], [dgr])
                    by, byr = bank("acc")
                    for (u0, t0, n) in ucol(tt):
                        for k in range(CONVK):
                            mm(by[:, t0:t0 + n], dg[:, k, :], U[:, cc, u0 - HALO + k:u0 - HALO + k + n],
                               k == 0, k == CONVK - 1, [dgr, Ur], byr)
                    y, yr = tmpf()
                    act(y, by, AF.Identity, [byr, Cr], [yr], bias=convp[:, l, 0, cc:cc + 1])
                    ys.append((y, yr))
                bm, bmr = bank("rot")
                for cc in range(4):
                    mm(bm, ones_f, ys[cc][0], cc == 0, cc == 3, [Cr, ys[cc][1]], bmr)
                bq, bqr = bank("rot")
                for cc in range(4):
                    sq, sqr_ = tmpb()
                    act(sq, ys[cc][0], AF.Square, [ys[cc][1]], [sqr_])
                    mm(bq, ones_b, sq, cc == 0, cc == 3, [Cr, sqr_], bqr)
                mean, meanr = tmpf()
                act(mean, bm, AF.Identity, [bmr], [meanr], scale=1.0 / 512)
                rs, rsr = tmpf()
                act(rs, mean, AF.Square, [meanr], [rsr])
                stt(rs, bq, 1.0 / 512, rs, ALU.mult, ALU.subtract, [bqr, rsr], [rsr])
                act(rs, rs, AF.Ln, [rsr], [rsr], bias=EPS)
                act(rs, rs, AF.Exp, [rsr], [rsr], scale=-0.5)
                for cc in range(4):
                    y, yr = ys[cc]
                    tt_op("dve", y, y, mean, ALU.subtract, [yr, meanr], [yr])
                    tt_op("dve", y, y, rs, ALU.mult, [yr, rsr], [yr])
                    act(Y[:, cc, tile_cols(tt)], y, AF.Silu, [yr, Cr], [Yr[tt]],
                        bias=convp[:, l, 2, cc:cc + 1], scale=convp[:, l, 1, cc:cc + 1])
            return Y, Yr

        nph = [0]

        def go():
            nph[0] += 1
            return nph[0] <= MAX_PHASES

        for l in range(2):
            if go():
                sw = make_mod_side_work(0) if l == 0 else None
                ffn_phase(l, 0, fuse_norm=(l, 1), barrier=(l == 0), side_work=sw)
            if go():
                Y, Yr = stage_A(l)
            if go():
                consume(l, 0, Y, Yr)
            if go():
                Y, Yr = stage_B(l)
            if go():
                consume(l, 1, Y, Yr, row_perm=lambda j, half: (j + 4 * half) * 64)
            if go():
                Y, Yr = stage_C(l)
            if go():
                consume(l, 2, Y, Yr, fuse_norm=(l, 2))
            if go():
                sw = make_mod_side_work(1) if l == 0 else None
                ffn_phase(l, 1, fuse_norm=((l + 1, 0) if l == 0 else None), side_work=sw, final_fuse=(l == 1))


        if not final_state["done"]:
            P.barrier()
            lreset()
            XN = [r3(vf(lalloc(16384), 4096), 8) for _ in range(2)]
            XNr = [Res("xn0"), Res("xn1")]
            sqs = [r3(vb(lalloc(8192), 4096), 8) for _ in range(2)]
            sqr = [Res("sq0"), Res("sq1")]
            ost = [vf(lalloc(4096), 1024) for _ in range(3)]
            ostr = [Res(f"ost{i}") for i in range(3)]
            oi = 0
            for tt in range(NT):
                cs = tile_cols(tt)
                sq, sr = sqs[tt % 2], sqr[tt % 2]
                xn, xnr = XN[tt % 2], XNr[tt % 2]
                act(sq, X[:, :, cs], AF.Square, [Xr[tt]], [sr])
                bk, bkr = bank("rot")
                for kc in range(KC):
                    mm(bk, ones_b, sq[:, kc, :], kc == 0, kc == KC - 1, [sr, Cr], bkr)
                rs, rsr = rsqrt_bank(bk, bkr, 1.0 / D, EPS)
                for kc in range(KC):
                    stt(xn[:, kc, :], X[:, kc, cs], fing[:, kc:kc + 1], rs, ALU.mult, ALU.mult, [Xr[tt], rsr, Cr], [xnr])
                for b in range(4):
                    o, orr = ost[oi % 3], ostr[oi % 3]
                    oi += 1
                    for g in range(2):
                        bk, bkr = bank("rot")
                        for j in range(4):
                            kc = 4 * g + j
                            P.op("pe", lambda e, bk=bk, j=j, xn=xn, kc=kc, b=b: e.transpose(
                                out=bk[:, j * 128:(j + 1) * 128], in_=xn[:, kc, b * 128:(b + 1) * 128], identity=ident_f),
                                reads=[xnr, Cr], writes=[bkr])
                        copy("act" if g == 0 else "dve", o[:, g * 512:(g + 1) * 512], bk, [bkr], [orr])
                    tb = tt * 4 + b
                    dst = ys_d[tb * 128:(tb + 1) * 128, :] if tb < 16 else yp_d[(tb - 16) * 128:(tb - 15) * 128, :]
                    P.dma("sp", f"fo{(oi - 1) % 3}", lambda e, o=o, dst=dst: e.dma_start(out=dst, in_=o), reads=[orr], writes=[])

        P.emit()
        build_nc.stats = {e: len(P.q[e]) for e in P.ENGS}
    return nc


def _rope_tables():
    T, dim = TS, 64
    rows, cols = np.meshgrid(np.arange(T // 64, dtype=np.float32), np.arange(64, dtype=np.float32), indexing="ij")
    inv = (10000.0 ** (-np.arange(0, dim // 2, 2, dtype=np.float32) / (dim // 2))).astype(np.float32)
    ang = np.stack([rows.reshape(-1, 1) * inv, cols.reshape(-1, 1) * inv], axis=1)
    ang = np.broadcast_to(ang[:, :, None, :], (T, 2, 2, dim // 4)).reshape(T, dim)
    cos = np.cos(ang).astype(np.float32).T
    sin = np.sin(ang).astype(np.float32).T
    return np.ascontiguousarray(np.concatenate([cos, cos], 0)), np.ascontiguousarray(np.concatenate([sin, sin], 0))


def _consts():
    ident = np.eye(128, dtype=np.float32)
    R = np.zeros((128, 128), np.float32)
    for blk in range(2):
        for a in range(2):
            for i in range(16):
                d0 = blk * 64 + a * 32 + i
                d1 = blk * 64 + a * 32 + 16 + i
                R[d0, d1] = -1.0
                R[d1, d0] = 1.0
    rotT = np.ascontiguousarray(R.T)
    ones = np.ones((128, 128), np.float32)
    p = np.arange(128)[:, None]
    f = np.arange(384)[None, :]
    mask = ((f - p >= 0) & (256 + p - f >= 0)).astype(np.float32)
    return np.ascontiguousarray(np.concatenate([ident, rotT, ones, mask], axis=1))


_NC_CACHE = {}


def kernel(x_prompt, x_sample, cache_diff_k, cache_diff_v, cache_win_k, cache_win_v, c, c_ctx,
           w_mod, b_mod, norm_g, ffn_w_in, ffn_w_out, w_in, diff_lambda, diff_subln_g, win_sink,
           conv_dw_w, conv_dw_b, conv_norm_g, conv_norm_b, w_branch, w_out, final_norm_g):
    f32 = lambda a: np.ascontiguousarray(np.asarray(a, dtype=np.float32))
    x_prompt, x_sample = f32(x_prompt), f32(x_sample)
    cosT, sinT = _rope_tables()
    cst = _consts()

    def fm(v, nchunk):
        return np.asarray(v, np.float32).reshape(nchunk, 128).T

    bmodT = np.ascontiguousarray(np.concatenate([fm(b_mod[l], 72) for l in range(2)], axis=1))
    normgT = np.ascontiguousarray(np.concatenate([fm(norm_g[l, i], 8) for l in range(2) for i in range(3)], axis=1))
    lamb = np.ascontiguousarray(np.broadcast_to(np.asarray(diff_lambda, np.float32).reshape(1, 512), (128, 512)))
    subgT = np.ascontiguousarray(np.asarray(diff_subln_g, np.float32).T)
    sink = np.asarray(win_sink, np.float32)
    sinkT = np.zeros((128, 8), np.float32)
    for l in range(2):
        for j in range(4):
            sinkT[0:64, l * 4 + j] = sink[l, j]
            sinkT[64:128, l * 4 + j] = sink[l, 4 + j]
    cw = np.asarray(conv_dw_w, np.float32)
    convwT = np.ascontiguousarray(
        cw.transpose(2, 0, 1).reshape(4, 128, 2, CONVK).transpose(1, 2, 0, 3).reshape(128, 2 * 4 * CONVK))
    cp = np.stack([np.asarray(conv_dw_b, np.float32), np.asarray(conv_norm_g, np.float32),
                   np.asarray(conv_norm_b, np.float32)], axis=1)
    convpT = np.ascontiguousarray(cp.reshape(2, 3, 4, 128).transpose(3, 0, 1, 2).reshape(128, 24))
    fingT = np.ascontiguousarray(fm(final_norm_g, 8))

    shared = {
        "w_mod": f32(w_mod), "bmodT": bmodT, "normgT": normgT, "ffn_w_in": f32(ffn_w_in), "ffn_w_out": f32(ffn_w_out),
        "w_in": f32(w_in), "lamb": lamb, "subgT": subgT, "sinkT": sinkT, "convwT": convwT, "convpT": convpT,
        "w_branch": f32(w_branch), "w_out": f32(w_out), "fingT": fingT, "cosT": cosT, "sinT": sinT, "cst": cst,
    }
    cdk = f32(cache_diff_k).reshape(8, 2, 256, 512)
    cdv = f32(cache_diff_v).reshape(8, 2, 256, 512)
    cwk = f32(cache_win_k).reshape(8, 2, 256, 128)
    cwv = f32(cache_win_v).reshape(8, 2, 256, 128)
    cc = np.asarray(c, np.float32)
    cctx = np.asarray(c_ctx, np.float32)
    in_maps = []
    for i in range(NCORES):
        condT = np.zeros((128, 8, 2), np.float32)
        condT[:, :, 0] = cc[i].reshape(8, 128).T
        condT[:, :, 1] = cctx.reshape(8, 128).T
        m = dict(shared)
        m.update({
            "xs": x_sample[i], "xp": np.ascontiguousarray(x_prompt[2 * i:2 * i + 2].reshape(512, D)),
            "cdk": cdk[i], "cdv": cdv[i], "cwk": cwk[i], "cwv": cwv[i],
            "condT": np.ascontiguousarray(condT.reshape(128, 16)),
        })
        in_maps.append(m)
    if "nc" not in _NC_CACHE:
        _NC_CACHE["nc"] = build_nc()
    nc = _NC_CACHE["nc"]
    res = run_bass_kernel_spmd(nc, in_maps, core_ids=list(range(NCORES)))
    r = res.results
    y_sample = np.stack([r[i]["ys"] for i in range(NCORES)], 0)
    y_prompt = np.concatenate([r[i]["yp"].reshape(2, 256, D) for i in range(NCORES)], 0)
    ndk = np.concatenate([r[i]["ndk"] for i in range(NCORES)], 0).reshape(16, 2, 256, 4, 2, 64)
    ndv = np.concatenate([r[i]["ndv"] for i in range(NCORES)], 0).reshape(16, 2, 256, 4, 128)
    nwk = np.concatenate([r[i]["nwk"] for i in range(NCORES)], 0).reshape(16, 2, 256, 2, 64)
    nwv = np.concatenate([r[i]["nwv"] for i in range(NCORES)], 0).reshape(16, 2, 256, 2, 64)
    return (y_prompt.astype(np.float32), y_sample.astype(np.float32), ndk.astype(np.float32),
            ndv.astype(np.float32), nwk.astype(np.float32), nwv.astype(np.float32))
```

```python
import math
from contextlib import ExitStack

import numpy as np
import concourse.bass as bass
import concourse.mybir as mybir
from concourse.bass_utils import run_bass_kernel_spmd

F32 = mybir.dt.float32
BF16 = mybir.dt.bfloat16
AF = mybir.ActivationFunctionType
ALU = mybir.AluOpType

D = 1024
KC = 8
NT = 5
TS = 2048
TOK = 2560
DFF = 2816
NG = 11
DIN = 6400
EPS = 1e-6
SUBEPS = 1e-5
SCALE = 0.125
NCORES = 8
MAX_PHASES = 1000
HALO = 15
CONVK = 31

C_QA, C_KA, C_VA, C_QB, C_KB, C_VB, C_CU, C_G = 0, 512, 1024, 1536, 2048, 2176, 2304, 3328

OFF_X = 0
OFF_N = 81920
OFF_C = 122880
OFF_RING = 130048
OFF_L = OFF_RING + 20480
ARENA_BYTES = 212480
LOCAL_BYTES = ARENA_BYTES - OFF_L


class Res:
    __slots__ = ("name", "w", "r", "excl")

    def __init__(self, name="", excl=False):
        self.name = name
        self.w = None
        self.r = {}
        self.excl = excl


class DSem:
    def __init__(self, sem):
        self.sem = sem
        self.count = 0


class Prog:
    ENGS = ("pe", "act", "dve", "pool", "sp")

    def __init__(self, nc, es):
        self.nc = nc
        self.es = es
        self.sem = {e: es.enter_context(nc.semaphore("s_" + e)) for e in self.ENGS}
        self.cnt = {e: 0 for e in self.ENGS}
        self.seen = {e: {} for e in self.ENGS}
        self.q = {e: [] for e in self.ENGS}
        self.bar = {e: {} for e in self.ENGS}
        self.dsem = {}
        self.ninstr = 0

    def semh(self, k):
        if isinstance(k, tuple):
            return self.dsem[k[1]].sem
        return self.sem[k]

    def _deps(self, eng, reads, writes, skip_same=True, own=None):
        need = dict(self.bar[eng])
        self.bar[eng] = {}

        def add(k, v):
            if skip_same and k == eng:
                return
            if own is not None and k == own:
                return
            if need.get(k, 0) < v:
                need[k] = v

        for r in reads:
            if r.w is not None:
                add(*r.w)
            if r.excl:
                for k, v in r.r.items():
                    if k != eng:
                        add(k, v)
        for w in writes:
            if w.w is not None:
                add(*w.w)
            for k, v in w.r.items():
                add(k, v)
        waits = []
        seen = self.seen[eng]
        for k, v in need.items():
            if skip_same and k == eng:
                continue
            if seen.get(k, 0) < v:
                seen[k] = v
                waits.append((k, v))
        return waits

    def op(self, eng, fn, reads=(), writes=()):
        waits = self._deps(eng, reads, writes, skip_same=(eng == "pe"))
        self.cnt[eng] += 1
        v = self.cnt[eng]
        self.q[eng].append((waits, fn, (eng, 1)))
        for r in reads:
            if r.r.get(eng, 0) < v:
                r.r[eng] = v
        for w in writes:
            w.w = (eng, v)
            w.r = {}
        self.ninstr += 1

    def dma(self, queue, key, fn, reads=(), writes=(), part=False):
        if key not in self.dsem:
            self.dsem[key] = DSem(self.es.enter_context(self.nc.semaphore("d_" + key)))
        ds = self.dsem[key]
        k = ("d", key)
        waits = self._deps(queue, reads, writes, skip_same=False, own=k)
        if not part and ds.count > 0 and self.seen[queue].get(k, 0) < ds.count:
            self.seen[queue][k] = ds.count
            waits.append((k, ds.count))
        ds.count += 16
        self.q[queue].append((waits, fn, (k, 16)))
        for r in reads:
            if r.r.get(k, 0) < ds.count:
                r.r[k] = ds.count
        for w in writes:
            w.w = (k, ds.count)
            w.r = {}
        self.ninstr += 1

    def barrier(self):
        toks = {e: self.cnt[e] for e in ("pe", "act", "dve", "pool") if self.cnt[e] > 0}
        for k, ds in self.dsem.items():
            if ds.count > 0:
                toks[("d", k)] = ds.count
        for e in self.ENGS:
            b = self.bar[e]
            for k, v in toks.items():
                if b.get(k, 0) < v:
                    b[k] = v

    def emit(self):
        nc = self.nc
        handles = {"pe": "tensor", "act": "scalar", "dve": "vector", "pool": "gpsimd", "sp": "sync"}
        final = []
        for k, ds in self.dsem.items():
            if ds.count > 0:
                final.append((("d", k), ds.count))
        for e in ("pe", "act", "dve", "pool"):
            if self.cnt[e] > 0:
                final.append((e, self.cnt[e]))
        with nc.Block() as block:
            for name in self.ENGS:
                lst = self.q[name]

                def body(e, lst=lst, name=name):
                    for waits, fn, inc in lst:
                        for (k, v) in waits:
                            e.wait_ge(self.semh(k), v)
                        ins = fn(e)
                        ins.then_inc(self.semh(inc[0]), inc[1])
                    if name == "sp":
                        for (k, v) in final:
                            e.wait_ge(self.semh(k), v)

                getattr(block, handles[name])(body)


def build_nc():
    nc = bass.Bass("TRN2", target_bir_lowering=False)

    def din(name, shape, dt=F32):
        return nc.dram_tensor(name, list(shape), dt, kind="ExternalInput").ap()

    def dout(name, shape):
        return nc.dram_tensor(name, list(shape), F32, kind="ExternalOutput").ap()

    xs_d = din("xs", [TS, D])
    xp_d = din("xp", [512, D])
    cdk_d = din("cdk", [2, 256, 512])
    cdv_d = din("cdv", [2, 256, 512])
    cwk_d = din("cwk", [2, 256, 128])
    cwv_d = din("cwv", [2, 256, 128])
    condT_d = din("condT", [128, 16])
    wmod_d = din("w_mod", [2, D, 9 * D])
    bmodT_d = din("bmodT", [128, 144])
    normgT_d = din("normgT", [128, 48])
    ffnwi_d = din("ffn_w_in", [2, 2, D, 2 * DFF])
    ffnwo_d = din("ffn_w_out", [2, 2, DFF, D])
    win_d = din("w_in", [2, D, DIN])
    lam_d = din("lamb", [128, 512])
    subg_d = din("subgT", [128, 2])
    sink_d = din("sinkT", [128, 8])
    convw_d = din("convwT", [128, 2 * 4 * CONVK])
    convp_d = din("convpT", [128, 24])
    wbr_d = din("w_branch", [2, 3, 512, D])
    wout_d = din("w_out", [2, D, D])
    fing_d = din("fingT", [128, 8])
    cos_d = din("cosT", [128, TS])
    sin_d = din("sinT", [128, TS])
    cst_d = din("cst", [128, 128 * 3 + 384])

    ys_d = dout("ys", [TS, D])
    yp_d = dout("yp", [512, D])
    ndk_d = dout("ndk", [2, 2, 256, 512])
    ndv_d = dout("ndv", [2, 2, 256, 512])
    nwk_d = dout("nwk", [2, 2, 256, 128])
    nwv_d = dout("nwv", [2, 2, 256, 128])

    es = ExitStack()
    with es:
        arena = es.enter_context(nc.sbuf_tensor("arena", [128, ARENA_BYTES // 4], F32))
        arena_b = arena.bitcast(BF16)
        banks = [es.enter_context(nc.psum_tensor(f"bank{i}", [128, 512], F32)) for i in range(8)]
        P = Prog(nc, es)

        def vf(off, n):
            assert off % 4 == 0
            return arena[:, off // 4: off // 4 + n]

        def vb(off, n):
            assert off % 2 == 0
            return arena_b[:, off // 2: off // 2 + n]

        def r3(ap, a):
            return ap.rearrange("p (a b) -> p a b", a=a)

        X = r3(vf(OFF_X, KC * TOK), KC)
        N = r3(vb(OFF_N, KC * TOK), KC)
        Xr = [Res(f"X{t}") for t in range(NT)]
        Nr = [Res(f"N{t}") for t in range(NT)]
        co = [OFF_C]

        def calloc(nbytes):
            o = co[0]
            co[0] += (nbytes + 31) // 32 * 32
            assert co[0] <= OFF_RING
            return o

        ident_f = vf(calloc(512), 128)
        rot_f = vf(calloc(512), 128)
        ones_f = vf(calloc(512), 128)
        ones_b = vb(calloc(256), 128)
        ident_b = vb(calloc(256), 128)
        mask_b = vb(calloc(768), 384)
        MOD = vf(calloc(1152), 288).rearrange("p (l v k c) -> p l v k c", l=2, v=9, k=8)
        DER = vf(calloc(640), 160).rearrange("p (l v k c) -> p l v k c", l=2, v=5, k=8)
        normg = vf(calloc(192), 48).rearrange("p (l i k) -> p l i k", l=2, i=3)
        fing = vf(calloc(32), 8)
        subg = vf(calloc(8), 2)
        gsub = vf(calloc(8), 2)
        neglam = vf(calloc(8), 2)
        sinkexp = vf(calloc(32), 8).rearrange("p (l j) -> p l j", l=2)
        convw = vf(calloc(992), 248).rearrange("p (l c k) -> p l c k", l=2, c=4)
        convp = vf(calloc(96), 24).rearrange("p (l t c) -> p l t c", l=2, t=3)
        bmod = vf(calloc(576), 144).rearrange("p (l j) -> p l j", l=2)
        scT = vf(calloc(64), 16).rearrange("p (k c) -> p k c", k=8)
        lamtmp = vf(calloc(32), 8)
        Cr = Res("consts")

        NTF, NTB = 8, 4
        tf_ap = [vf(OFF_RING + i * 2048, 512) for i in range(NTF)]
        tf_res = [Res(f"tf{i}") for i in range(NTF)]
        tb_ap = [vb(OFF_RING + NTF * 2048 + i * 1024, 512) for i in range(NTB)]
        tb_res = [Res(f"tb{i}") for i in range(NTB)]
        ring_i = {"f": 0, "b": 0, "nf": NTF}

        def tmpf():
            i = ring_i["f"] % ring_i["nf"]
            ring_i["f"] = (i + 1) % ring_i["nf"]
            return tf_ap[i], tf_res[i]

        def tmpb():
            i = ring_i["b"]
            ring_i["b"] = (i + 1) % NTB
            return tb_ap[i], tb_res[i]

        bank_res = [Res(f"bank{i}", excl=True) for i in range(8)]
        bank_i = {"rot": 0, "acc": 0}
        ROT = (0, 1, 2, 3)
        acc_pool = {"lst": (4, 5, 6, 7)}

        def bank(pool="rot"):
            lst = ROT if pool == "rot" else acc_pool["lst"]
            i = lst[bank_i[pool] % len(lst)]
            bank_i[pool] += 1
            return banks[i][:, :], bank_res[i]

        lo = [OFF_L]

        def lreset(keep=0):
            lo[0] = OFF_L + keep

        def lalloc(nbytes):
            o = lo[0]
            lo[0] += (nbytes + 31) // 32 * 32
            assert lo[0] <= ARENA_BYTES, (lo[0], ARENA_BYTES)
            return o

        def tile_cols(tt):
            return slice(tt * 512, (tt + 1) * 512)

        def mm(out, lhsT, rhs, start, stop, reads, wres):
            P.op("pe", lambda e: e.matmul(out, lhsT=lhsT, rhs=rhs, start=start, stop=stop),
                 reads=reads, writes=[wres])

        def act(out, in_, func, reads, writes, bias=None, scale=None):
            kw = {}
            if bias is not None:
                kw["bias"] = bias
            if scale is not None:
                kw["scale"] = scale
            P.op("act", lambda e: e.activation(out=out, in_=in_, func=func, **kw), reads=reads, writes=writes)

        def tt_op(eng, out, in0, in1, op, reads, writes):
            P.op(eng, lambda e: e.tensor_tensor(out=out, in0=in0, in1=in1, op=op), reads=reads, writes=writes)

        def stt(out, in0, scalar, in1, op0, op1, reads, writes):
            P.op("dve", lambda e: e.scalar_tensor_tensor(out=out, in0=in0, scalar=scalar, in1=in1, op0=op0, op1=op1),
                 reads=reads, writes=writes)

        def ts_op(eng, out, in0, s1, s2, op0, op1, reads, writes):
            if op1 is None:
                P.op(eng, lambda e: e.tensor_scalar(out=out, in0=in0, scalar1=s1, scalar2=None, op0=op0),
                     reads=reads, writes=writes)
            else:
                P.op(eng, lambda e: e.tensor_scalar(out=out, in0=in0, scalar1=s1, scalar2=s2, op0=op0, op1=op1),
                     reads=reads, writes=writes)

        def copy(eng, out, in_, reads, writes):
            if eng == "act":
                P.op("act", lambda e: e.copy(out=out, in_=in_), reads=reads, writes=writes)
            else:
                P.op(eng, lambda e: e.tensor_copy(out=out, in_=in_), reads=reads, writes=writes)

        def rsqrt_bank(bk, bkr, scale, eps):
            t1, t1r = tmpf()
            act(t1, bk, AF.Ln, [bkr], [t1r], bias=eps, scale=scale)
            t2, t2r = tmpf()
            act(t2, t1, AF.Exp, [t1r], [t2r], scale=-0.5)
            return t2, t2r

        def ld(key, out, in_, wres, queue="sp"):
            P.dma(queue, key, lambda e: e.dma_start(out=out, in_=in_), reads=[], writes=[wres], part=True)

        cst_t = vf(lalloc(768 * 4), 768)
        ld("c0", cst_t, cst_d, Cr)
        ld("c0", bmod.rearrange("p l j -> p (l j)"), bmodT_d, Cr)
        ld("c0", normg.rearrange("p l i k -> p (l i k)"), normgT_d, Cr)
        ld("c0", fing, fing_d, Cr)
        ld("c0", subg, subg_d, Cr)
        ld("c0", sinkexp.rearrange("p l j -> p (l j)"), sink_d, Cr)
        ld("c0", convw.rearrange("p l c k -> p (l c k)"), convw_d, Cr)
        ld("c0", convp.rearrange("p l t c -> p (l t c)"), convp_d, Cr)
        ld("c0", scT.rearrange("p k c -> p (k c)"), condT_d, Cr)
        lam_t = vf(lalloc(2048), 512)
        ld("c0", lam_t, lam_d, Cr)
        copy("dve", ident_f, cst_t[:, 0:128], [Cr], [Cr])
        copy("dve", rot_f, cst_t[:, 128:256], [Cr], [Cr])
        copy("dve", ones_f, cst_t[:, 256:384], [Cr], [Cr])
        copy("dve", ones_b, cst_t[:, 256:384], [Cr], [Cr])
        copy("dve", ident_b, cst_t[:, 0:128], [Cr], [Cr])
        copy("dve", mask_b, cst_t[:, 384:768], [Cr], [Cr])
        act(scT.rearrange("p k c -> p (k c)"), scT.rearrange("p k c -> p (k c)"), AF.Silu, [Cr], [Cr])
        act(sinkexp.rearrange("p l j -> p (l j)"), sinkexp.rearrange("p l j -> p (l j)"), AF.Exp, [Cr], [Cr])
        for l in range(2):
            lam_init = 0.8 - 0.6 * math.exp(-0.3 * l)
            base = l * 256
            pr = vf(lalloc(1024), 256)
            tt_op("dve", pr[:, 0:64], lam_t[:, base:base + 64], lam_t[:, base + 64:base + 128], ALU.mult, [Cr], [Cr])
            tt_op("dve", pr[:, 64:128], lam_t[:, base + 128:base + 192], lam_t[:, base + 192:base + 256], ALU.mult, [Cr], [Cr])
            P.op("dve", lambda e, pr=pr: e.reduce_sum(out=lamtmp[:, 0:1], in_=pr[:, 0:64], axis=mybir.AxisListType.X),
                 reads=[Cr], writes=[Cr])
            P.op("dve", lambda e, pr=pr: e.reduce_sum(out=lamtmp[:, 1:2], in_=pr[:, 64:128], axis=mybir.AxisListType.X),
                 reads=[Cr], writes=[Cr])
            act(lamtmp[:, 2:4], lamtmp[:, 0:2], AF.Exp, [Cr], [Cr])
            tt_op("dve", lamtmp[:, 4:5], lamtmp[:, 3:4], lamtmp[:, 2:3], ALU.subtract, [Cr], [Cr])
            ts_op("dve", neglam[:, l:l + 1], lamtmp[:, 4:5], -lam_init, None, ALU.add, None, [Cr], [Cr])
            ts_op("dve", gsub[:, l:l + 1], subg[:, l:l + 1], 1.0 - lam_init, None, ALU.mult, None, [Cr], [Cr])

        NWM = 3
        wm_slab = [r3(vb(lalloc(8192), 4096), 8) for _ in range(NWM)]
        wm_res = [Res(f"wm{i}") for i in range(NWM)]
        scTb = vb(calloc(32), 16).rearrange("p (k c) -> p k c", k=8)
        copy("dve", scTb, scT, [Cr], [Cr])

        def mod_slab(l, s, bk, bkr, sl, slr, key):
            P.dma("pool", key,
                  lambda e: e.dma_start(
                      out=sl, in_=wmod_d[l, :, s * 512:(s + 1) * 512].rearrange("(kc p) n -> p kc n", p=128)),
                  reads=[], writes=[slr])
            for fb in range(4):
                j = s * 4 + fb
                for kc in range(KC):
                    mm(bk[:, 2 * j:2 * j + 2], sl[:, kc, fb * 128:(fb + 1) * 128], scTb[:, kc, :],
                       kc == 0, kc == KC - 1, [slr, Cr], bkr)

        def mod_finish(l, bk, bkr, v_lo, v_hi):
            j0, j1 = v_lo * 8, v_hi * 8
            mod_l = MOD[:, l, v_lo:v_hi].rearrange("p v k c -> p (v k) c")
            bmv = bmod[:, l, j0:j1]
            bm_b = bass.AP(bmod.tensor, bmv.offset, [list(bmv.ap[0]), [1, j1 - j0], [0, 2]])
            tt_op("dve", mod_l, bk[:, 2 * j0:2 * j1].rearrange("p (j c) -> p j c", c=2), bm_b, ALU.add, [bkr, Cr], [Cr])
            for i in range(3):
                if v_lo <= 3 * i + 1 < v_hi:
                    g_b = bass.AP(normg.tensor, normg[:, l, i, :].offset, [list(normg[:, l, i, :].ap[0]), [1, 8], [0, 2]])
                    stt(DER[:, l, i], MOD[:, l, 3 * i + 1], 1.0, g_b, ALU.add, ALU.mult, [Cr], [Cr])
            if v_lo <= 2 < v_hi:
                ts_op("dve", DER[:, l, 3], MOD[:, l, 2], 0.5, None, ALU.mult, None, [Cr], [Cr])
            if v_lo <= 8 < v_hi:
                ts_op("dve", DER[:, l, 4], MOD[:, l, 8], 0.5, None, ALU.mult, None, [Cr], [Cr])

        bk0, bk0r = banks[7][:, :], bank_res[7]
        for s in range(6):
            mod_slab(0, s, bk0, bk0r, wm_slab[s % NWM], wm_res[s % NWM], f"wm{s % NWM}")
        mod_finish(0, bk0, bk0r, 0, 3)

        def make_mod_side_work(which):
            top = ARENA_BYTES - 2 * 8192
            sl2 = [r3(vb(top + i * 8192, 4096), 8) for i in range(2)]
            sl2r = [Res("wmB0"), Res("wmB1")]
            b7, b7r = banks[7][:, :], bank_res[7]
            work = []
            k = 0
            if which == 0:
                for s in range(6, 18):
                    work.append(lambda s=s, k=k: mod_slab(0, s, b7, b7r, sl2[k % 2], sl2r[k % 2], f"wmB{k % 2}"))
                    k += 1
                work.append(lambda: mod_finish(0, b7, b7r, 3, 9))
            else:
                for s in range(18):
                    work.append(lambda s=s, k=k: mod_slab(1, s, b7, b7r, sl2[k % 2], sl2r[k % 2], f"wmB{k % 2}"))
                    k += 1
                work.append(lambda: mod_finish(1, b7, b7r, 0, 9))
            return work

        def mcol(l, v, kc, cond):
            return MOD[:, l, v, kc, cond:cond + 1]

        def dcol(l, v, kc, cond):
            return DER[:, l, v, kc, cond:cond + 1]

        def norm_tile(l, i, tt):
            cond = 0 if tt < 4 else 1
            cs = tile_cols(tt)
            bk, bkr = bank("rot")
            for kc in range(KC):
                sqt, sqr_ = tmpf()
                sqb = sqt.bitcast(BF16)[:, 0:512]
                act(sqb, X[:, kc, cs], AF.Square, [Xr[tt]], [sqr_])
                mm(bk, ones_b, sqb, kc == 0, kc == KC - 1, [sqr_, Cr], bkr)
            rs, rsr = rsqrt_bank(bk, bkr, 1.0 / D, EPS)
            tl = [tmpf(), tmpf()]
            for kc in range(KC):
                t, tr = tl[kc % 2]
                tt_op("dve", t, X[:, kc, cs], rs, ALU.mult, [Xr[tt], rsr], [tr])
                act(N[:, kc, cs], t, AF.Identity, [tr, Cr], [Nr[tt]],
                    bias=mcol(l, 3 * i, kc, cond), scale=dcol(l, i, kc, cond))

        def norm_phase(l, i):
            for tt in range(NT):
                norm_tile(l, i, tt)

        xst = [vf(lalloc(4096), 1024) for _ in range(3)]
        xst_res = [Res(f"xst{i}") for i in range(3)]
        for tb in range(20):
            st, str_ = xst[tb % 3], xst_res[tb % 3]
            src = xs_d[tb * 128:(tb + 1) * 128, :] if tb < 16 else xp_d[(tb - 16) * 128:(tb - 15) * 128, :]
            P.dma("sp", f"xst{tb % 3}", lambda e, st=st, src=src: e.dma_start(out=st, in_=src), reads=[], writes=[str_])
            for g in range(2):
                bk, bkr = bank("rot")
                for j in range(4):
                    kc = 4 * g + j
                    P.op("pe", lambda e, bk=bk, j=j, st=st, kc=kc: e.transpose(
                        out=bk[:, j * 128:(j + 1) * 128], in_=st[:, kc * 128:(kc + 1) * 128], identity=ident_f),
                        reads=[str_, Cr], writes=[bkr])
                copy("act" if g == 0 else "dve", X[:, 4 * g:4 * g + 4, tb * 128:(tb + 1) * 128],
                     bk.rearrange("p (a b) -> p a b", a=4), [bkr], [Xr[tb // 4]])
            if tb % 4 == 3:
                norm_tile(0, 0, tb // 4)


        ffn_keep = {}

        final_state = {"done": False}

        def ffn_phase(l, f, fuse_norm=None, barrier=True, side_work=None, final_fuse=False):
            NS = 2 if final_fuse else 3
            if barrier or not ffn_keep:
                P.barrier()
                ffn_keep["wir"] = [Res(f"wi{i}") for i in range(NS)]
                ffn_keep["wor"] = [Res(f"wo{i}") for i in range(NS)]
                ffn_keep["hr"] = [Res("h0"), Res("h1")]
            lreset()
            wi = [r3(vb(lalloc(8192), 4096), 8) for _ in range(NS)]
            wo = [r3(vb(lalloc(4096), 2048), 2) for _ in range(NS)]
            wir = ffn_keep["wir"]
            wor = ffn_keep["wor"]
            hb = [r3(vb(lalloc(2048), 1024), 2) for _ in range(2)]
            hr = ffn_keep["hr"]
            gv = 3 if f == 0 else 4
            hi = 0
            fin_pend = []
            if final_fuse:
                xn = r3(vf(lalloc(16384), 4096), 8)
                xnr = Res("xnF")
                ostF = [vf(lalloc(4096), 1024) for _ in range(2)]
                ostFr = [Res("ostF0"), Res("ostF1")]
                foi = [0]
                final_state["done"] = True

                def final_part_a(tt):
                    cs = tile_cols(tt)
                    bk, bkr = bank("rot")
                    for kc in range(KC):
                        sqt, sqr_ = tmpf()
                        sqb = sqt.bitcast(BF16)[:, 0:512]
                        act(sqb, X[:, kc, cs], AF.Square, [Xr[tt]], [sqr_])
                        mm(bk, ones_b, sqb, kc == 0, kc == KC - 1, [sqr_, Cr], bkr)
                    rs, rsr = rsqrt_bank(bk, bkr, 1.0 / D, EPS)
                    for kc in range(KC):
                        stt(xn[:, kc, :], X[:, kc, cs], fing[:, kc:kc + 1], rs, ALU.mult, ALU.mult,
                            [Xr[tt], rsr, Cr], [xnr])

                def final_part_b(tt):
                    for b in range(4):
                        o, orr = ostF[foi[0] % 2], ostFr[foi[0] % 2]
                        key = f"foF{foi[0] % 2}"
                        foi[0] += 1
                        for g2 in range(2):
                            bk, bkr = bank("rot")
                            for j in range(4):
                                kc = 4 * g2 + j
                                P.op("pe", lambda e, bk=bk, j=j, kc=kc, b=b: e.transpose(
                                    out=bk[:, j * 128:(j + 1) * 128], in_=xn[:, kc, b * 128:(b + 1) * 128],
                                    identity=ident_f), reads=[xnr, Cr], writes=[bkr])
                            copy("act" if g2 == 0 else "dve", o[:, g2 * 512:(g2 + 1) * 512], bk, [bkr], [orr])
                        tb = tt * 4 + b
                        dst = ys_d[tb * 128:(tb + 1) * 128, :] if tb < 16 else yp_d[(tb - 16) * 128:(tb - 15) * 128, :]
                        P.dma("sp", key, lambda e, o=o, dst=dst: e.dma_start(out=dst, in_=o), reads=[orr], writes=[])
            if side_work:
                acc_pool["lst"] = (4, 5, 6)

            def load_group(g):
                s = g % NS
                w_i, w_o = wi[s], wo[s]
                P.dma("pool", f"wi{s}", lambda e: e.dma_start(
                    out=w_i[:, :, 0:256], in_=ffnwi_d[l, f, :, 256 * g:256 * g + 256].rearrange("(kc p) n -> p kc n", p=128)),
                    reads=[], writes=[wir[s]])
                P.dma("pool", f"wi{s}", lambda e: e.dma_start(
                    out=w_i[:, :, 256:512],
                    in_=ffnwi_d[l, f, :, DFF + 256 * g:DFF + 256 * g + 256].rearrange("(kc p) n -> p kc n", p=128)),
                    reads=[], writes=[wir[s]], part=True)
                P.dma("pool", f"wo{s}", lambda e: e.dma_start(
                    out=w_o, in_=ffnwo_d[l, f, 256 * g:256 * g + 256, :].rearrange("(j p) n -> p j n", p=128)),
                    reads=[], writes=[wor[s]])

            def second_stage(s, tt, h, hrr):
                cond = 0 if tt < 4 else 1
                cs = tile_cols(tt)
                w_o = wo[s]
                for fo in range(KC):
                    bo, bor = bank("acc")
                    for j in range(2):
                        mm(bo, w_o[:, j, fo * 128:(fo + 1) * 128], h[:, j, :], j == 0, j == 1, [wor[s], hrr], bor)
                    stt(X[:, fo, cs], bo, dcol(l, gv, fo, cond), X[:, fo, cs], ALU.mult, ALU.add,
                        [bor, Cr, Xr[tt]], [Xr[tt]])

            prev = None
            load_group(0)
            for g in range(NG):
                s = g % NS
                w_i = wi[s]
                if g + 1 < NG and NS == 3:
                    load_group(g + 1)
                if g == NG - 1:
                    while side_work:
                        side_work.pop(0)()
                for tt in range(NT):
                    if side_work and tt in (1, 2, 3):
                        side_work.pop(0)()
                    cs = tile_cols(tt)
                    h, hrr = hb[hi % 2], hr[hi % 2]
                    hi += 1
                    for j in range(2):
                        ba, bar_ = bank("rot")
                        for kc in range(KC):
                            mm(ba, w_i[:, kc, j * 128:(j + 1) * 128], N[:, kc, cs], kc == 0, kc == KC - 1,
                               [wir[s], Nr[tt]], bar_)
                        bb, bbr = bank("rot")
                        for kc in range(KC):
                            mm(bb, w_i[:, kc, 256 + j * 128:256 + (j + 1) * 128], N[:, kc, cs], kc == 0, kc == KC - 1,
                               [wir[s], Nr[tt]], bbr)
                        sg, sgr = tmpf()
                        act(sg, ba, AF.Silu, [bar_], [sgr])
                        tt_op("dve", h[:, j, :], sg, bb, ALU.mult, [sgr, bbr], [hrr])
                    while fin_pend:
                        fin_pend.pop(0)()
                    if prev is not None:
                        second_stage(*prev)
                        if fuse_norm is not None and g == NG - 1 and tt >= 1:
                            norm_tile(fuse_norm[0], fuse_norm[1], prev[1])
                        if final_fuse and g == NG - 1 and tt >= 1:
                            final_part_a(prev[1])
                            fin_pend.append(lambda t_=prev[1]: final_part_b(t_))
                    if NS == 2 and tt == 0 and g + 1 < NG:
                        load_group(g + 1)
                    prev = (s, tt, h, hrr)
            second_stage(*prev)
            while fin_pend:
                fin_pend.pop(0)()
            if final_fuse:
                final_part_a(prev[1])
                final_part_b(prev[1])
            while side_work:
                side_work.pop(0)()
            acc_pool["lst"] = (4, 5, 6, 7)
            if fuse_norm is not None:
                norm_tile(fuse_norm[0], fuse_norm[1], prev[1])

        Y_BYTES = 4 * TOK * 2

        def rope_store(bk, bkr, dst, dres, tabs, eng2="pool", split=None):
            cosT, sinT, tabr = tabs
            qf, qfr = tmpf()
            copy("act", qf, bk, [bkr], [qfr])
            br, brr = bank("rot")
            mm(br, rot_f, qf, True, True, [qfr, Cr], brr)
            t1, t1r = tmpf()
            tt_op(eng2, t1, qf, cosT, ALU.mult, [qfr, tabr], [t1r])
            t2, t2r = tmpf()
            tt_op("dve", t2, br, sinT, ALU.mult, [brr, tabr], [t2r])
            if split is None:
                tt_op("dve", dst, t1, t2, ALU.add, [t1r, t2r], [dres])
            else:
                d0, d1 = split
                tt_op("dve", d0[0:64, :], t1[0:64, :], t2[0:64, :], ALU.add, [t1r, t2r], [dres])
                tt_op("dve", d1[64:128, :], t1[64:128, :], t2[64:128, :], ALU.add, [t1r, t2r], [dres])

        def load_tabs(tabbuf, tabres, idx, tt):
            cosT, sinT = tabbuf[idx % 2]
            tr = tabres[idx % 2]
            P.dma("sp", f"tab{idx % 2}", lambda e: e.dma_start(out=cosT, in_=cos_d[:, tt * 512:(tt + 1) * 512]),
                  reads=[], writes=[tr])
            P.dma("sp", f"tab{idx % 2}", lambda e: e.dma_start(out=sinT, in_=sin_d[:, tt * 512:(tt + 1) * 512]),
                  reads=[], writes=[tr], part=True)
            return cosT, sinT, tr

        ost_i = [0]

        def store_rows(src_ap, src_res, dst_ap):
            k = f"ost{ost_i[0] % 4}"
            ost_i[0] += 1
            P.dma("sp", k, lambda e: e.dma_start(out=dst_ap, in_=src_ap), reads=[src_res], writes=[])

        def consume(l, bi, Y, Yr, row_perm=None, fuse_norm=None):
            P.barrier()
            lreset(Y_BYTES)
            wg = r3(vb(lalloc(16384), 8192), 8)
            wb_ = r3(vb(lalloc(8192), 4096), 4)
            wo = r3(vb(lalloc(16384), 8192), 8)
            m = r3(vb(OFF_RING + (NTF - 2) * 2048, 4096), 8)
            ring_i["nf"] = NTF - 2
            wor, mr = Res("wo"), Res("m")
            wgr = [Res(f"wg{i}") for i in range(KC)]
            wbr = [Res("wbA"), Res("wbB")]
            c0 = C_G + bi * D

            def load_wg(ob):
                P.dma("pool", f"cwg{ob}", lambda e: e.dma_start(
                    out=wg[:, :, ob * 128:(ob + 1) * 128],
                    in_=win_d[l, :, c0 + ob * 128:c0 + (ob + 1) * 128].rearrange("(kc p) n -> p kc n", p=128)),
                    reads=[], writes=[wgr[ob]])

            def load_wb(hf):
                cs_ = slice(hf * 512, (hf + 1) * 512)
                if row_perm is None:
                    P.dma("pool", f"cwb{hf}", lambda e: e.dma_start(
                        out=wb_[:, :, cs_], in_=wbr_d[l, bi, :, cs_].rearrange("(kc p) n -> p kc n", p=128)),
                        reads=[], writes=[wbr[hf]])
                else:
                    for j in range(4):
                        for half in range(2):
                            r0 = row_perm(j, half)
                            P.dma("pool", f"cwb{hf}", lambda e, j=j, half=half, r0=r0: e.dma_start(
                                out=wb_[half * 64:(half + 1) * 64, j, cs_], in_=wbr_d[l, bi, r0:r0 + 64, cs_]),
                                reads=[], writes=[wbr[hf]], part=(j + half > 0))

            load_wg(0)
            load_wb(0)
            for ob in range(1, 4):
                load_wg(ob)
            load_wb(1)
            for ob in range(4, KC):
                load_wg(ob)
            P.dma("pool", "cwo", lambda e: e.dma_start(
                out=wo, in_=wout_d[l].rearrange("(kc p) n -> p kc n", p=128)), reads=[], writes=[wor])
            for tt in range(NT):
                cond = 0 if tt < 4 else 1
                cs = tile_cols(tt)
                for ob in range(KC):
                    bg, bgr = bank("rot")
                    for kc in range(KC):
                        mm(bg, wg[:, kc, ob * 128:(ob + 1) * 128], N[:, kc, cs], kc == 0, kc == KC - 1,
                           [wgr[ob], Nr[tt]], bgr)
                    bp, bpr = bank("rot")
                    for kc in range(4):
                        mm(bp, wb_[:, kc, ob * 128:(ob + 1) * 128], Y[:, kc, cs], kc == 0, kc == 3,
                           [wbr[ob // 4], Yr[tt]], bpr)
                    sg, sgr = tmpf()
                    act(sg, bg, AF.Sigmoid, [bgr], [sgr])
                    tt_op("dve", m[:, ob, :], sg, bp, ALU.mult, [sgr, bpr], [mr])
                    if fuse_norm is not None and tt >= 1 and ob == 3:
                        norm_tile(fuse_norm[0], fuse_norm[1], tt - 1)
                for fo in range(KC):
                    bo, bor = bank("acc")
                    for ob in range(KC):
                        mm(bo, wo[:, ob, fo * 128:(fo + 1) * 128], m[:, ob, :], ob == 0, ob == KC - 1,
                           [wor, mr], bor)
                    stt(X[:, fo, cs], bo, mcol(l, 5, fo, cond), X[:, fo, cs], ALU.mult, ALU.add,
                        [bor, Cr, Xr[tt]], [Xr[tt]])
            if fuse_norm is not None:
                norm_tile(fuse_norm[0], fuse_norm[1], NT - 1)
            P.barrier()
            ring_i["nf"] = NTF
            ring_i["f"] = 0

        def stage_A(l):
            P.barrier()
            lreset()
            Y = r3(vb(lalloc(Y_BYTES), 4 * TOK), 4)
            Yr = [Res(f"Y{t}") for t in range(NT)]
            QTp = [vb(lalloc(4096), 2048) for _ in range(2)]
            KT = vb(lalloc(4608), 2304)
            V = r3(vb(lalloc(4608), 2304), 18)
            QTr, KTr, Vr = Res("QT"), Res("KT"), Res("V")
            P.op("pool", lambda e: e.memset(QTp[0][64:128, :], 0.0), reads=[], writes=[QTr])
            P.op("pool", lambda e: e.memset(QTp[1][0:64, :], 0.0), reads=[], writes=[QTr])
            wh = [r3(vb(lalloc(6144), 3072), 8) for _ in range(2)]
            whr = [Res("wh0"), Res("wh1")]
            tabbuf = [(vf(lalloc(2048), 512), vf(lalloc(2048), 512)) for _ in range(2)]
            tabres = [Res("tab0"), Res("tab1")]
            tab_i = 0

            pend = []

            def flush():
                while pend:
                    pend.pop(0)()

            LOOK = 3

            def attn(qcols, kchunks, ycols, h):
                nq = qcols.stop - qcols.start
                acc = [(bank("acc"), bank("acc")) for _ in range(2)]
                items = [(c, kc, idx) for idx, kc in enumerate(kchunks) for c in range(2)]
                n = len(items)
                nk = len(kchunks)
                es_ = {}

                def s_stage(i):
                    c, kc, idx = items[i]
                    ps = slice(c * 64, (c + 1) * 64)
                    bs, bsr = bank("rot")
                    mm(bs[:, 0:nq], KT[:, kc * 128:(kc + 1) * 128], QTp[c][:, qcols], True, True, [KTr, QTr], bsr)
                    e_, er = tmpb()
                    act(e_[:, 0:nq], bs[:, 0:nq], AF.Exp, [bsr], [er], scale=SCALE)
                    es_[i] = (e_, er)

                ZG = 9
                ngrp = (nk + ZG - 1) // ZG
                zst = {}
                zpend = []

                def av_stage(i):
                    c, kc, idx = items[i]
                    (bu, bur), (bz, bzr) = acc[c]
                    e_, er = es_.pop(i)
                    while zpend:
                        zpend.pop(0)()
                    mm(bu[:, 0:nq], V[:, kc, :], e_[:, 0:nq], idx == 0, idx == nk - 1, [Vr, er], bur)
                    g, pos = divmod(idx, ZG)
                    gsize = min(ZG, nk - g * ZG)
                    if gsize == 1:
                        mm(bz[:, 0:nq], ones_b, e_[:, 0:nq], g == 0, g == ngrp - 1, [Cr, er], bzr)
                        return
                    if pos == 0:
                        t, tr = tmpf()
                        zb = t.bitcast(BF16)[:, 0:512]
                        zst[c] = (zb, tr)
                        copy("dve", zb[:, 0:nq], e_[:, 0:nq], [er], [tr])
                    else:
                        zb, tr = zst[c]
                        tt_op("dve", zb[:, 0:nq], zb[:, 0:nq], e_[:, 0:nq], ALU.add, [tr, er], [tr])
                    if pos == gsize - 1:
                        zpend.append(lambda zb=zb, tr=tr, g=g, bz=bz, bzr=bzr: mm(
                            bz[:, 0:nq], ones_b, zb[:, 0:nq], g == 0, g == ngrp - 1, [Cr, tr], bzr))

                for i in range(n + LOOK):
                    if i < n:
                        s_stage(i)
                    if i == LOOK - 1:
                        flush()
                    if i >= LOOK:
                        av_stage(i - LOOK)
                while zpend:
                    zpend.pop(0)()
                ts_ = []
                ev = []
                for c in range(2):
                    (bu, bur), (bz, bzr) = acc[c]
                    uc, ucr = tmpf()
                    copy("dve", uc[:, 0:nq], bu[:, 0:nq], [bur], [ucr])
                    zc, zcr = tmpf()
                    copy("dve", zc[:, 0:nq], bz[:, 0:nq], [bzr], [zcr])
                    ev.append((uc, ucr, zc, zcr))
                for c in range(2):
                    uc, ucr, zc, zcr = ev[c]
                    act(zc[:, 0:nq], zc[:, 0:nq], AF.Ln, [zcr], [zcr])
                    act(zc[:, 0:nq], zc[:, 0:nq], AF.Exp, [zcr], [zcr], scale=-1.0)
                    tt_op("dve", uc[:, 0:nq], uc[:, 0:nq], zc[:, 0:nq], ALU.mult, [ucr, zcr], [ucr])
                    ts_.append((uc, ucr))
                o, orr = tmpf()
                stt(o[:, 0:nq], ts_[1][0][:, 0:nq], neglam[:, l:l + 1], ts_[0][0][:, 0:nq], ALU.mult, ALU.add,
                    [ts_[0][1], ts_[1][1], Cr], [orr])
                sq, sqr_ = tmpb()
                act(sq[:, 0:nq], o[:, 0:nq], AF.Square, [orr], [sqr_])
                t2, t2r = tmpf()

                def part2():
                    bs, bsr = bank("rot")
                    mm(bs[:, 0:nq], ones_b, sq[:, 0:nq], True, True, [Cr, sqr_], bsr)
                    act(t2[:, 0:nq], bs[:, 0:nq], AF.Ln, [bsr], [t2r], bias=SUBEPS, scale=1.0 / 128)
                    act(t2[:, 0:nq], t2[:, 0:nq], AF.Exp, [t2r], [t2r], scale=-0.5)
                    stt(Y[:, h, ycols], o[:, 0:nq], gsub[:, l:l + 1], t2[:, 0:nq], ALU.mult, ALU.mult,
                        [orr, t2r, Cr], [Yr[ycols.start // 512]])

                pend.append(part2)

            for h in range(4):
                w_, wr = wh[h % 2], whr[h % 2]
                for part, c0 in enumerate((C_QA, C_KA, C_VA)):
                    P.dma("pool", f"wh{h % 2}", lambda e, w_=w_, part=part, c0=c0, h=h: e.dma_start(
                        out=w_[:, :, part * 128:(part + 1) * 128],
                        in_=win_d[l, :, c0 + h * 128:c0 + (h + 1) * 128].rearrange("(kc p) n -> p kc n", p=128)),
                        reads=[], writes=[wr], part=(part > 0))
                ck, ckr = tmpf()
                ck3 = ck[:, 0:256].rearrange("p (a b) -> p a b", a=2)
                P.dma("sp", "ck", lambda e, ck3=ck3, h=h: e.dma_start(
                    out=ck3, in_=cdk_d[l, :, h * 128:(h + 1) * 128].rearrange("(a p) n -> p a n", p=128)),
                    reads=[], writes=[ckr])
                bk, bkr = bank("rot")
                for a in range(2):
                    P.op("pe", lambda e, bk=bk, a=a, ck3=ck3: e.transpose(
                        out=bk[:, a * 128:(a + 1) * 128], in_=ck3[:, a, :], identity=ident_f),
                        reads=[ckr, Cr], writes=[bkr])
                copy("dve", KT[:, 0:256], bk[:, 0:256], [bkr], [KTr])
                P.dma("pool", "cv", lambda e, h=h: e.dma_start(
                    out=V[:, 0:2, :], in_=cdv_d[l, :, h * 128:(h + 1) * 128].rearrange("(a p) n -> p a n", p=128)),
                    reads=[], writes=[Vr])
                for tt in range(4):
                    cs = tile_cols(tt)
                    tabs = load_tabs(tabbuf, tabres, tab_i, tt)
                    tab_i += 1
                    bq, bqr = bank("rot")
                    for kc in range(KC):
                        mm(bq, w_[:, kc, 0:128], N[:, kc, cs], kc == 0, kc == KC - 1, [wr, Nr[tt]], bqr)
                    bk, bkr = bank("rot")
                    for kc in range(KC):
                        mm(bk, w_[:, kc, 128:256], N[:, kc, cs], kc == 0, kc == KC - 1, [wr, Nr[tt]], bkr)
                    bv, bvr = bank("rot")
                    for b in range(4):
                        for kc in range(KC):
                            mm(bv[:, b * 128:(b + 1) * 128], N[:, kc, tt * 512 + b * 128:tt * 512 + (b + 1) * 128],
                               w_[:, kc, 256:384], kc == 0, kc == KC - 1, [wr, Nr[tt]], bvr)
                    rope_store(bq, bqr, None, QTr, tabs, split=(QTp[0][:, cs], QTp[1][:, cs]))
                    rope_store(bk, bkr, KT[:, 256 + tt * 512:256 + (tt + 1) * 512], KTr, tabs)
                    copy("act", V[:, 2 + 4 * tt:6 + 4 * tt, :], bv.rearrange("p (a b) -> p a b", a=4), [bvr], [Vr])
                for qt in range(4):
                    attn(tile_cols(qt), list(range(18)), tile_cols(qt), h)
                flush()
                cs = tile_cols(4)
                bq, bqr = bank("rot")
                for kc in range(KC):
                    mm(bq, w_[:, kc, 0:128], N[:, kc, cs], kc == 0, kc == KC - 1, [wr, Nr[4]], bqr)
                copy("act", QTp[0][0:64, 0:512], bq[0:64, :], [bqr], [QTr])
                copy("act", QTp[1][64:128, 0:512], bq[64:128, :], [bqr], [QTr])
                bk, bkr = bank("rot")
                for kc in range(KC):
                    mm(bk, w_[:, kc, 128:256], N[:, kc, cs], kc == 0, kc == KC - 1, [wr, Nr[4]], bkr)
                copy("dve", KT[:, 0:512], bk, [bkr], [KTr])
                for part, dst in ((1, ndk_d), (2, ndv_d)):
                    bv, bvr = bank("rot")
                    for b in range(4):
                        for kc in range(KC):
                            mm(bv[:, b * 128:(b + 1) * 128], N[:, kc, 2048 + b * 128:2048 + (b + 1) * 128],
                               w_[:, kc, part * 128:(part + 1) * 128], kc == 0, kc == KC - 1, [wr, Nr[4]], bvr)
                    of, ofr = tmpf()
                    copy("act", of, bv, [bvr], [ofr])
                    if part == 2:
                        copy("dve", V[:, 0:4, :], bv.rearrange("p (a b) -> p a b", a=4), [bvr], [Vr])
                    for s in range(2):
                        store_rows(of.rearrange("p (a b) -> p a b", a=4)[:, 2 * s:2 * s + 2, :], ofr,
                                   dst[s, l, :, h * 128:(h + 1) * 128].rearrange("(a p) n -> p a n", p=128))
                for s in range(2):
                    attn(slice(s * 256, (s + 1) * 256), [2 * s, 2 * s + 1], slice(2048 + s * 256, 2048 + (s + 1) * 256), h)
                flush()
            return Y, Yr

        def stage_B(l):
            P.barrier()
            lreset()
            Y = r3(vb(lalloc(Y_BYTES), 4 * TOK), 4)
            Yr = [Res(f"Y{t}") for t in range(NT)]
            KT = vb(lalloc(4608), 2304)
            V = r3(vb(lalloc(4608), 2304), 18)
            KTr, Vr = Res("KTb"), Res("Vb")
            QTp = [vb(lalloc(4096), 2048) for _ in range(2)]
            QTpr = Res("QTb")
            P.op("pool", lambda e: e.memset(QTp[0][64:128, :], 0.0), reads=[], writes=[QTpr])
            P.op("pool", lambda e: e.memset(QTp[1][0:64, :], 0.0), reads=[], writes=[QTpr])
            wkv = r3(vb(lalloc(4096), 2048), 8)
            wkvr = Res("wkv")
            wq = [r3(vb(lalloc(2048), 1024), 8) for _ in range(2)]
            wqr = [Res("wq0"), Res("wq1")]
            tabbuf = [(vf(lalloc(2048), 512), vf(lalloc(2048), 512)) for _ in range(2)]
            tabres = [Res("tab0"), Res("tab1")]
            tab_i = 0
            P.dma("pool", "wkv", lambda e: e.dma_start(
                out=wkv, in_=win_d[l, :, C_KB:C_KB + 256].rearrange("(kc p) n -> p kc n", p=128)), reads=[], writes=[wkvr])
            ck, ckr = tmpf()
            ck3 = ck[:, 0:256].rearrange("p (a b) -> p a b", a=2)
            P.dma("sp", "ck", lambda e: e.dma_start(
                out=ck3, in_=cwk_d[l].rearrange("(a p) n -> p a n", p=128)), reads=[], writes=[ckr])
            bk, bkr = bank("rot")
            for a in range(2):
                P.op("pe", lambda e, bk=bk, a=a: e.transpose(
                    out=bk[:, a * 128:(a + 1) * 128], in_=ck3[:, a, :], identity=ident_f),
                    reads=[ckr, Cr], writes=[bkr])
            copy("dve", KT[:, 0:256], bk[:, 0:256], [bkr], [KTr])
            P.dma("pool", "cv", lambda e: e.dma_start(
                out=V[:, 0:2, :], in_=cwv_d[l].rearrange("(a p) n -> p a n", p=128)), reads=[], writes=[Vr])
            for tt in range(4):
                cs = tile_cols(tt)
                tabs = load_tabs(tabbuf, tabres, tab_i, tt)
                tab_i += 1
                bk, bkr = bank("rot")
                for kc in range(KC):
                    mm(bk, wkv[:, kc, 0:128], N[:, kc, cs], kc == 0, kc == KC - 1, [wkvr, Nr[tt]], bkr)
                bv, bvr = bank("rot")
                for b in range(4):
                    for kc in range(KC):
                        mm(bv[:, b * 128:(b + 1) * 128], N[:, kc, tt * 512 + b * 128:tt * 512 + (b + 1) * 128],
                           wkv[:, kc, 128:256], kc == 0, kc == KC - 1, [wkvr, Nr[tt]], bvr)
                rope_store(bk, bkr, KT[:, 256 + tt * 512:256 + (tt + 1) * 512], KTr, tabs)
                copy("act", V[:, 2 + 4 * tt:6 + 4 * tt, :], bv.rearrange("p (a b) -> p a b", a=4), [bvr], [Vr])

            LOOK = 3

            def attn_pair(j, qcols, ctx_chunks, local, ycols):
                nq = qcols.stop - qcols.start
                acc = [(bank("acc"), bank("acc")) for _ in range(2)]
                base = [(kc, 0, nq, None) for kc in ctx_chunks] + local
                nb = len(base)
                items = [(half, idx) for idx in range(nb) for half in range(2)]
                n = len(items)
                es_ = {}

                def s_stage(i):
                    half, idx = items[i]
                    kc, c_lo, c_hi, m_lo = base[idx]
                    w = c_hi - c_lo
                    bs, bsr = bank("rot")
                    mm(bs[:, 0:w], KT[:, kc * 128:(kc + 1) * 128], QTp[half][:, qcols.start + c_lo:qcols.start + c_hi],
                       True, True, [KTr, QTpr], bsr)
                    e_, er = tmpb()
                    act(e_[:, 0:w], bs[:, 0:w], AF.Exp, [bsr], [er], scale=SCALE)
                    if m_lo is not None:
                        tt_op("dve", e_[:, 0:w], e_[:, 0:w], mask_b[:, m_lo:m_lo + w], ALU.mult, [er, Cr], [er])
                    es_[i] = (e_, er)

                def av_stage(i):
                    half, idx = items[i]
                    kc, c_lo, c_hi, m_lo = base[idx]
                    w = c_hi - c_lo
                    (bu, bur), (bz, bzr) = acc[half]
                    e_, er = es_.pop(i)
                    mm(bu[:, c_lo:c_hi], V[:, kc, :], e_[:, 0:w], idx == 0, idx == nb - 1, [Vr, er], bur)
                    mm(bz[:, c_lo:c_hi], ones_b, e_[:, 0:w], idx == 0, idx == nb - 1, [Cr, er], bzr)

                for i in range(n + LOOK):
                    if i < n:
                        s_stage(i)
                    if i >= LOOK:
                        av_stage(i - LOOK)
                for half in range(2):
                    ps = slice(half * 64, (half + 1) * 64)
                    (bu, bur), (bz, bzr) = acc[half]
                    t1, t1r = tmpf()
                    act(t1[ps, 0:nq], bz[ps, 0:nq], AF.Ln, [bzr, Cr], [t1r], bias=sinkexp[ps, l, j:j + 1])
                    act(t1[ps, 0:nq], t1[ps, 0:nq], AF.Exp, [t1r], [t1r], scale=-1.0)
                    tt_op("dve", Y[ps, j, ycols], bu[ps, 0:nq], t1[ps, 0:nq], ALU.mult, [bur, t1r],
                          [Yr[ycols.start // 512]])

            for j in range(4):
                w_, wr = wq[j % 2], wqr[j % 2]
                for half in range(2):
                    hh = j + 4 * half
                    P.dma("pool", f"wq{j % 2}", lambda e, w_=w_, half=half, hh=hh: e.dma_start(
                        out=w_[:, :, half * 64:(half + 1) * 64],
                        in_=win_d[l, :, C_QB + hh * 64:C_QB + (hh + 1) * 64].rearrange("(kc p) n -> p kc n", p=128)),
                        reads=[], writes=[wr], part=(half > 0))
                for tt in range(4):
                    cs = tile_cols(tt)
                    tabs = load_tabs(tabbuf, tabres, tab_i, tt)
                    tab_i += 1
                    bq, bqr = bank("rot")
                    for kc in range(KC):
                        mm(bq, w_[:, kc, :], N[:, kc, cs], kc == 0, kc == KC - 1, [wr, Nr[tt]], bqr)
                    rope_store(bq, bqr, None, QTpr, tabs, split=(QTp[0][:, cs], QTp[1][:, cs]))
                for qt in range(4):
                    local = []
                    for jj in range(max(0, 4 * qt - 1), min(15, 4 * qt + 4) + 1):
                        qb_lo = max(4 * qt, jj - 1)
                        qb_hi = min(4 * qt + 3, jj + 1)
                        local.append((2 + jj, (qb_lo - 4 * qt) * 128, (qb_hi - 4 * qt + 1) * 128,
                                      (qb_lo - (jj - 1)) * 128))
                    attn_pair(j, tile_cols(qt), [0, 1], local, tile_cols(qt))
            cs = tile_cols(4)
            bk, bkr = bank("rot")
            for kc in range(KC):
                mm(bk, wkv[:, kc, 0:128], N[:, kc, cs], kc == 0, kc == KC - 1, [wkvr, Nr[4]], bkr)
            copy("dve", KT[:, 0:512], bk, [bkr], [KTr])
            for part, dst in ((0, nwk_d), (1, nwv_d)):
                bv, bvr = bank("rot")
                for b in range(4):
                    for kc in range(KC):
                        mm(bv[:, b * 128:(b + 1) * 128], N[:, kc, 2048 + b * 128:2048 + (b + 1) * 128],
                           wkv[:, kc, part * 128:(part + 1) * 128], kc == 0, kc == KC - 1, [wkvr, Nr[4]], bvr)
                of, ofr = tmpf()
                copy("act", of, bv, [bvr], [ofr])
                if part == 1:
                    copy("dve", V[:, 0:4, :], bv.rearrange("p (a b) -> p a b", a=4), [bvr], [Vr])
                for s in range(2):
                    store_rows(of.rearrange("p (a b) -> p a b", a=4)[:, 2 * s:2 * s + 2, :], ofr,
                               dst[s, l].rearrange("(a p) n -> p a n", p=128))
            for j in range(4):
                w_, wr = wq[j % 2], wqr[j % 2]
                for half in range(2):
                    hh = j + 4 * half
                    P.dma("pool", f"wq{j % 2}", lambda e, w_=w_, half=half, hh=hh: e.dma_start(
                        out=w_[:, :, half * 64:(half + 1) * 64],
                        in_=win_d[l, :, C_QB + hh * 64:C_QB + (hh + 1) * 64].rearrange("(kc p) n -> p kc n", p=128)),
                        reads=[], writes=[wr], part=(half > 0))
                bq, bqr = bank("rot")
                for kc in range(KC):
                    mm(bq, w_[:, kc, :], N[:, kc, cs], kc == 0, kc == KC - 1, [wr, Nr[4]], bqr)
                copy("act", QTp[0][0:64, 0:512], bq[0:64, :], [bqr], [QTpr])
                copy("act", QTp[1][64:128, 0:512], bq[64:128, :], [bqr], [QTpr])
                for s in range(2):
                    attn_pair(j, slice(s * 256, (s + 1) * 256), [2 * s, 2 * s + 1], [],
                              slice(2048 + s * 256, 2048 + (s + 1) * 256))
            return Y, Yr

        def stage_C(l):
            P.barrier()
            lreset()
            Y = r3(vb(lalloc(Y_BYTES), 4 * TOK), 4)
            Yr = [Res(f"Y{t}") for t in range(NT)]
            UW = 2656
            U = r3(vb(lalloc(4 * UW * 2), 4 * UW), 4)
            Ur = Res("U")
            mark = lo[0]
            wc = [r3(vb(lalloc(4096), 2048), 8) for _ in range(2)]
            wcr = [Res("wc0"), Res("wc1")]
            P.op("pool", lambda e: e.memset(U.rearrange("p a b -> p (a b)"), 0.0), reads=[], writes=[Ur])

            def ucol(tt):
                if tt < 4:
                    return [(HALO + tt * 512, 0, 512)]
                return [(2078 + HALO, 0, 256), (2078 + 286 + HALO, 256, 256)]

            for cc in range(4):
                w_, wr = wc[cc % 2], wcr[cc % 2]
                for part in range(2):
                    c0 = C_CU + part * 512 + cc * 128
                    P.dma("pool", f"wc{cc % 2}", lambda e, w_=w_, part=part, c0=c0: e.dma_start(
                        out=w_[:, :, part * 128:(part + 1) * 128],
                        in_=win_d[l, :, c0:c0 + 128].rearrange("(kc p) n -> p kc n", p=128)), reads=[], writes=[wr],
                        part=(part > 0))
                for tt in range(NT):
                    cs = tile_cols(tt)
                    ba, bar_ = bank("rot")
                    for kc in range(KC):
                        mm(ba, w_[:, kc, 0:128], N[:, kc, cs], kc == 0, kc == KC - 1, [wr, Nr[tt]], bar_)
                    bg, bgr = bank("rot")
                    for kc in range(KC):
                        mm(bg, w_[:, kc, 128:256], N[:, kc, cs], kc == 0, kc == KC - 1, [wr, Nr[tt]], bgr)
                    sg, sgr = tmpf()
                    act(sg, bg, AF.Sigmoid, [bgr], [sgr])
                    for (u0, t0, n) in ucol(tt):
                        tt_op("dve", U[:, cc, u0:u0 + n], ba[:, t0:t0 + n], sg[:, t0:t0 + n], ALU.mult,
                              [bar_, sgr], [Ur])
            P.barrier()
            lo[0] = mark
            DG = [r3(vb(lalloc(CONVK * 128 * 2), CONVK * 128), CONVK) for _ in range(2)]
            DGr = [Res("dg0"), Res("dg1")]
            di = 0
            for tt in range(NT):
                ys = []
                for cc in range(4):
                    dg, dgr = DG[di % 2], DGr[di % 2]
                    di += 1
                    idb = bass.AP(ident_b.tensor, ident_b.offset, [list(ident_b.ap[0]), [0, CONVK], [1, 128]])
                    wv = convw[:, l, cc, :]
                    wbc = bass.AP(wv.tensor, wv.offset, [list(wv.ap[0]), [1, CONVK], [0, 128]])
                    tt_op("pool", dg, idb, wbc, ALU.mult, [Cr], [dgr])
                    by, byr = bank("acc")
                    for (u0, t0, n) in ucol(tt):
                        for k in range(CONVK):
                            mm(by[:, t0:t0 + n], dg[:, k, :], U[:, cc, u0 - HALO + k:u0 - HALO + k + n],
                               k == 0, k == CONVK - 1, [dgr, Ur], byr)
                    y, yr = tmpf()
                    act(y, by, AF.Identity, [byr, Cr], [yr], bias=convp[:, l, 0, cc:cc + 1])
                    ys.append((y, yr))
                bm, bmr = bank("rot")
                for cc in range(4):
                    mm(bm, ones_f, ys[cc][0], cc == 0, cc == 3, [Cr, ys[cc][1]], bmr)
                bq, bqr = bank("rot")
                for cc in range(4):
                    sq, sqr_ = tmpb()
                    act(sq, ys[cc][0], AF.Square, [ys[cc][1]], [sqr_])
                    mm(bq, ones_b, sq, cc == 0, cc == 3, [Cr, sqr_], bqr)
                mean, meanr = tmpf()
                act(mean, bm, AF.Identity, [bmr], [meanr], scale=1.0 / 512)
                rs, rsr = tmpf()
                act(rs, mean, AF.Square, [meanr], [rsr])
                stt(rs, bq, 1.0 / 512, rs, ALU.mult, ALU.subtract, [bqr, rsr], [rsr])
                act(rs, rs, AF.Ln, [rsr], [rsr], bias=EPS)
                act(rs, rs, AF.Exp, [rsr], [rsr], scale=-0.5)
                for cc in range(4):
                    y, yr = ys[cc]
                    tt_op("dve", y, y, mean, ALU.subtract, [yr, meanr], [yr])
                    tt_op("dve", y, y, rs, ALU.mult, [yr, rsr], [yr])
                    act(Y[:, cc, tile_cols(tt)], y, AF.Silu, [yr, Cr], [Yr[tt]],
                        bias=convp[:, l, 2, cc:cc + 1], scale=convp[:, l, 1, cc:cc + 1])
            return Y, Yr

        nph = [0]

        def go():
            nph[0] += 1
            return nph[0] <= MAX_PHASES

        for l in range(2):
            if go():
                sw = make_mod_side_work(0) if l == 0 else None
                ffn_phase(l, 0, fuse_norm=(l, 1), barrier=(l == 0), side_work=sw)
            if go():
                Y, Yr = stage_A(l)
            if go():
                consume(l, 0, Y, Yr)
            if go():
                Y, Yr = stage_B(l)
            if go():
                consume(l, 1, Y, Yr, row_perm=lambda j, half: (j + 4 * half) * 64)
            if go():
                Y, Yr = stage_C(l)
            if go():
                consume(l, 2, Y, Yr, fuse_norm=(l, 2))
            if go():
                sw = make_mod_side_work(1) if l == 0 else None
                ffn_phase(l, 1, fuse_norm=((l + 1, 0) if l == 0 else None), side_work=sw, final_fuse=(l == 1))


        if not final_state["done"]:
            P.barrier()
            lreset()
            XN = [r3(vf(lalloc(16384), 4096), 8) for _ in range(2)]
            XNr = [Res("xn0"), Res("xn1")]
            sqs = [r3(vb(lalloc(8192), 4096), 8) for _ in range(2)]
            sqr = [Res("sq0"), Res("sq1")]
            ost = [vf(lalloc(4096), 1024) for _ in range(3)]
            ostr = [Res(f"ost{i}") for i in range(3)]
            oi = 0
            for tt in range(NT):
                cs = tile_cols(tt)
                sq, sr = sqs[tt % 2], sqr[tt % 2]
                xn, xnr = XN[tt % 2], XNr[tt % 2]
                act(sq, X[:, :, cs], AF.Square, [Xr[tt]], [sr])
                bk, bkr = bank("rot")
                for kc in range(KC):
                    mm(bk, ones_b, sq[:, kc, :], kc == 0, kc == KC - 1, [sr, Cr], bkr)
                rs, rsr = rsqrt_bank(bk, bkr, 1.0 / D, EPS)
                for kc in range(KC):
                    stt(xn[:, kc, :], X[:, kc, cs], fing[:, kc:kc + 1], rs, ALU.mult, ALU.mult, [Xr[tt], rsr, Cr], [xnr])
                for b in range(4):
                    o, orr = ost[oi % 3], ostr[oi % 3]
                    oi += 1
                    for g in range(2):
                        bk, bkr = bank("rot")
                        for j in range(4):
                            kc = 4 * g + j
                            P.op("pe", lambda e, bk=bk, j=j, xn=xn, kc=kc, b=b: e.transpose(
                                out=bk[:, j * 128:(j + 1) * 128], in_=xn[:, kc, b * 128:(b + 1) * 128], identity=ident_f),
                                reads=[xnr, Cr], writes=[bkr])
                        copy("act" if g == 0 else "dve", o[:, g * 512:(g + 1) * 512], bk, [bkr], [orr])
                    tb = tt * 4 + b
                    dst = ys_d[tb * 128:(tb + 1) * 128, :] if tb < 16 else yp_d[(tb - 16) * 128:(tb - 15) * 128, :]
                    P.dma("sp", f"fo{(oi - 1) % 3}", lambda e, o=o, dst=dst: e.dma_start(out=dst, in_=o), reads=[orr], writes=[])

        P.emit()
        build_nc.stats = {e: len(P.q[e]) for e in P.ENGS}
    return nc


def _rope_tables():
    T, dim = TS, 64
    rows, cols = np.meshgrid(np.arange(T // 64, dtype=np.float32), np.arange(64, dtype=np.float32), indexing="ij")
    inv = (10000.0 ** (-np.arange(0, dim // 2, 2, dtype=np.float32) / (dim // 2))).astype(np.float32)
    ang = np.stack([rows.reshape(-1, 1) * inv, cols.reshape(-1, 1) * inv], axis=1)
    ang = np.broadcast_to(ang[:, :, None, :], (T, 2, 2, dim // 4)).reshape(T, dim)
    cos = np.cos(ang).astype(np.float32).T
    sin = np.sin(ang).astype(np.float32).T
    return np.ascontiguousarray(np.concatenate([cos, cos], 0)), np.ascontiguousarray(np.concatenate([sin, sin], 0))


def _consts():
    ident = np.eye(128, dtype=np.float32)
    R = np.zeros((128, 128), np.float32)
    for blk in range(2):
        for a in range(2):
            for i in range(16):
                d0 = blk * 64 + a * 32 + i
                d1 = blk * 64 + a * 32 + 16 + i
                R[d0, d1] = -1.0
                R[d1, d0] = 1.0
    rotT = np.ascontiguousarray(R.T)
    ones = np.ones((128, 128), np.float32)
    p = np.arange(128)[:, None]
    f = np.arange(384)[None, :]
    mask = ((f - p >= 0) & (256 + p - f >= 0)).astype(np.float32)
    return np.ascontiguousarray(np.concatenate([ident, rotT, ones, mask], axis=1))


_NC_CACHE = {}


def kernel(x_prompt, x_sample, cache_diff_k, cache_diff_v, cache_win_k, cache_win_v, c, c_ctx,
           w_mod, b_mod, norm_g, ffn_w_in, ffn_w_out, w_in, diff_lambda, diff_subln_g, win_sink,
           conv_dw_w, conv_dw_b, conv_norm_g, conv_norm_b, w_branch, w_out, final_norm_g):
    f32 = lambda a: np.ascontiguousarray(np.asarray(a, dtype=np.float32))
    x_prompt, x_sample = f32(x_prompt), f32(x_sample)
    cosT, sinT = _rope_tables()
    cst = _consts()

    def fm(v, nchunk):
        return np.asarray(v, np.float32).reshape(nchunk, 128).T

    bmodT = np.ascontiguousarray(np.concatenate([fm(b_mod[l], 72) for l in range(2)], axis=1))
    normgT = np.ascontiguousarray(np.concatenate([fm(norm_g[l, i], 8) for l in range(2) for i in range(3)], axis=1))
    lamb = np.ascontiguousarray(np.broadcast_to(np.asarray(diff_lambda, np.float32).reshape(1, 512), (128, 512)))
    subgT = np.ascontiguousarray(np.asarray(diff_subln_g, np.float32).T)
    sink = np.asarray(win_sink, np.float32)
    sinkT = np.zeros((128, 8), np.float32)
    for l in range(2):
        for j in range(4):
            sinkT[0:64, l * 4 + j] = sink[l, j]
            sinkT[64:128, l * 4 + j] = sink[l, 4 + j]
    cw = np.asarray(conv_dw_w, np.float32)
    convwT = np.ascontiguousarray(
        cw.transpose(2, 0, 1).reshape(4, 128, 2, CONVK).transpose(1, 2, 0, 3).reshape(128, 2 * 4 * CONVK))
    cp = np.stack([np.asarray(conv_dw_b, np.float32), np.asarray(conv_norm_g, np.float32),
                   np.asarray(conv_norm_b, np.float32)], axis=1)
    convpT = np.ascontiguousarray(cp.reshape(2, 3, 4, 128).transpose(3, 0, 1, 2).reshape(128, 24))
    fingT = np.ascontiguousarray(fm(final_norm_g, 8))

    shared = {
        "w_mod": f32(w_mod), "bmodT": bmodT, "normgT": normgT, "ffn_w_in": f32(ffn_w_in), "ffn_w_out": f32(ffn_w_out),
        "w_in": f32(w_in), "lamb": lamb, "subgT": subgT, "sinkT": sinkT, "convwT": convwT, "convpT": convpT,
        "w_branch": f32(w_branch), "w_out": f32(w_out), "fingT": fingT, "cosT": cosT, "sinT": sinT, "cst": cst,
    }
    cdk = f32(cache_diff_k).reshape(8, 2, 256, 512)
    cdv = f32(cache_diff_v).reshape(8, 2, 256, 512)
    cwk = f32(cache_win_k).reshape(8, 2, 256, 128)
    cwv = f32(cache_win_v).reshape(8, 2, 256, 128)
    cc = np.asarray(c, np.float32)
    cctx = np.asarray(c_ctx, np.float32)
    in_maps = []
    for i in range(NCORES):
        condT = np.zeros((128, 8, 2), np.float32)
        condT[:, :, 0] = cc[i].reshape(8, 128).T
        condT[:, :, 1] = cctx.reshape(8, 128).T
        m = dict(shared)
        m.update({
            "xs": x_sample[i], "xp": np.ascontiguousarray(x_prompt[2 * i:2 * i + 2].reshape(512, D)),
            "cdk": cdk[i], "cdv": cdv[i], "cwk": cwk[i], "cwv": cwv[i],
            "condT": np.ascontiguousarray(condT.reshape(128, 16)),
        })
        in_maps.append(m)
    if "nc" not in _NC_CACHE:
        _NC_CACHE["nc"] = build_nc()
    nc = _NC_CACHE["nc"]
    res = run_bass_kernel_spmd(nc, in_maps, core_ids=list(range(NCORES)))
    r = res.results
    y_sample = np.stack([r[i]["ys"] for i in range(NCORES)], 0)
    y_prompt = np.concatenate([r[i]["yp"].reshape(2, 256, D) for i in range(NCORES)], 0)
    ndk = np.concatenate([r[i]["ndk"] for i in range(NCORES)], 0).reshape(16, 2, 256, 4, 2, 64)
    ndv = np.concatenate([r[i]["ndv"] for i in range(NCORES)], 0).reshape(16, 2, 256, 4, 128)
    nwk = np.concatenate([r[i]["nwk"] for i in range(NCORES)], 0).reshape(16, 2, 256, 2, 64)
    nwv = np.concatenate([r[i]["nwv"] for i in range(NCORES)], 0).reshape(16, 2, 256, 2, 64)
    return (y_prompt.astype(np.float32), y_sample.astype(np.float32), ndk.astype(np.float32),
            ndv.astype(np.float32), nwk.astype(np.float32), nwv.astype(np.float32))
```
